# Optimizing a Trainium2 kernel written in Bass

```python
import jax, jax.numpy as jnp
from jax import lax
import numpy as np

D_MODEL = 1024
BATCH = 8
SEQ = 2048
DEPTH = 1

GRID_W = 64
N_ATT_HEADS = 8
ATT_HEAD_DIM = 64
D_ATT = N_ATT_HEADS * ATT_HEAD_DIM
WIN_H_MAX = 8
WIN_W = 16
Q_BLOCK_W = 16
KV_BLOCK_W = Q_BLOCK_W + WIN_W
N_COL_BLOCKS = GRID_W // Q_BLOCK_W
D_REC = D_MODEL
N_REC_BLOCKS = 16
REC_BLOCK = D_REC // N_REC_BLOCKS
CONV_W = 4
LRU_C = 8.0
N_DIR = 2
D_FF = 4 * D_MODEL
EPS = 1e-6
D_IN = 3 * D_ATT + 2 * D_REC + 2 * D_MODEL
SPLITS = [int(v) for v in np.cumsum([D_ATT, D_ATT, D_ATT, D_REC, D_REC, D_MODEL])]

kernel_name = "hybrid_natten_rglru_gated_encoder"


def rms_norm(x, g):
    x32 = x.astype(jnp.float32)
    y = x32 * lax.rsqrt(jnp.mean(x32 * x32, axis=-1, keepdims=True) + EPS)
    return (y * g.astype(jnp.float32)).astype(x.dtype)


def neighbourhood_attention(q, k, v, rpb):
    b, s, h, dh = q.shape
    rows = s // GRID_W
    kh = min(WIN_H_MAX, rows)
    r = np.arange(rows)
    row_start = np.clip(r - kh // 2, 0, rows - kh)
    row_idx = row_start[:, None] + np.arange(kh)[None, :]
    n = np.arange(N_COL_BLOCKS)
    col_start = np.clip(n * Q_BLOCK_W - WIN_W // 2, 0, GRID_W - KV_BLOCK_W)
    col_idx = col_start[:, None] + np.arange(KV_BLOCK_W)[None, :]
    qc = n[:, None] * Q_BLOCK_W + np.arange(Q_BLOCK_W)[None, :]
    win_start = np.clip(qc - WIN_W // 2, 0, GRID_W - WIN_W)
    kc = col_idx[:, None, :]
    valid = (kc >= win_start[..., None]) & (kc < win_start[..., None] + WIN_W)
    d_row = row_idx - r[:, None] + (WIN_H_MAX - 1)
    d_col = np.clip(kc - qc[..., None], -(WIN_W - 1), WIN_W - 1) + (WIN_W - 1)

    scale = ATT_HEAD_DIM ** -0.5
    q_blk = (q * scale).reshape(b, rows, N_COL_BLOCKS, Q_BLOCK_W, h, dh).transpose(0, 4, 1, 2, 3, 5)
    k_grid = k.reshape(b, rows, GRID_W, h, dh).transpose(0, 3, 1, 2, 4)
    v_grid = v.reshape(b, rows, GRID_W, h, dh).transpose(0, 3, 1, 2, 4)
    ri = row_idx[:, None, :, None]
    ci = col_idx[None, :, None, :]
    kg = k_grid[:, :, ri, ci]
    vg = v_grid[:, :, ri, ci]

    scores = jnp.einsum('bhrnqd,bhrnikd->bhrnqik', q_blk, kg).astype(jnp.float32)
    bias = rpb.astype(jnp.float32)[:, d_row[:, None, None, :, None], d_col[None, :, :, None, :]]
    scores = scores + bias[None]
    scores = jnp.where(valid[:, :, None, :], scores, -1e30)
    probs = jax.nn.softmax(scores, axis=(-2, -1)).astype(v.dtype)
    out = jnp.einsum('bhrnqik,bhrnikd->bhrnqd', probs, vg)
    return out.transpose(0, 2, 3, 4, 1, 5).reshape(b, s, h * dh)


def centred_depthwise_conv(u, w, bias):
    s = u.shape[1]
    left = CONV_W // 2
    right = CONV_W - 1 - left
    up = jnp.pad(u, ((0, 0), (left, right), (0, 0)))
    out = bias
    for j in range(CONV_W):
        out = out + up[:, j:j + s] * w[j]
    return out


def block_diag_linear(u, w, b):
    bsz, s, c = u.shape
    ub = u.reshape(bsz, s, N_REC_BLOCKS, REC_BLOCK)
    return jnp.einsum('bsnc,ncd->bsnd', ub, w).reshape(bsz, s, c) + b


def rg_lru(u, w_a, b_a, w_i, b_i, lam, reverse):
    r_gate = jax.nn.sigmoid(block_diag_linear(u, w_a, b_a)).astype(jnp.float32)
    i_gate = jax.nn.sigmoid(block_diag_linear(u, w_i, b_i))
    log_a = -LRU_C * r_gate * jax.nn.softplus(-lam.astype(jnp.float32))
    a = jnp.exp(log_a)
    mult = jnp.sqrt(jnp.maximum(-jnp.expm1(2.0 * log_a), 0.0))
    bx = mult * (i_gate * u).astype(jnp.float32)

    def combine(c1, c2):
        a1, b1 = c1
        a2, b2 = c2
        return a1 * a2, a2 * b1 + b2

    _, h = lax.associative_scan(combine, (a, bx), axis=1, reverse=reverse)
    return h.astype(u.dtype)


def setup_inputs(seed: int = 0) -> dict:
    key = jax.random.key(seed)
    ks = jax.random.split(key, 20)
    f32 = jnp.float32
    nrm = lambda k, shape, fan_in: jax.random.normal(k, shape, f32) * (fan_in ** -0.5)
    x = jax.random.normal(ks[0], (BATCH, SEQ, D_MODEL), f32)
    ln1_g = 1.0 + 0.05 * jax.random.normal(ks[1], (DEPTH, D_MODEL), f32)
    w_in = nrm(ks[2], (DEPTH, D_MODEL, D_IN), D_MODEL)
    b_in = 0.02 * jax.random.normal(ks[3], (DEPTH, D_IN), f32)
    rpb = 0.02 * jax.random.normal(ks[4], (DEPTH, N_ATT_HEADS, 2 * WIN_H_MAX - 1, 2 * WIN_W - 1), f32)
    w_att_o = nrm(ks[5], (DEPTH, D_ATT, D_MODEL), D_ATT)
    conv_w = nrm(ks[6], (DEPTH, CONV_W, D_REC), CONV_W)
    conv_b = 0.02 * jax.random.normal(ks[7], (DEPTH, D_REC), f32)
    w_rg_a = nrm(ks[8], (DEPTH, N_DIR, N_REC_BLOCKS, REC_BLOCK, REC_BLOCK), REC_BLOCK)
    b_rg_a = 0.02 * jax.random.normal(ks[9], (DEPTH, N_DIR, D_REC), f32)
    w_rg_i = nrm(ks[10], (DEPTH, N_DIR, N_REC_BLOCKS, REC_BLOCK, REC_BLOCK), REC_BLOCK)
    b_rg_i = 0.02 * jax.random.normal(ks[11], (DEPTH, N_DIR, D_REC), f32)
    a_c = jax.random.uniform(ks[12], (DEPTH, N_DIR, D_REC), f32, 0.9, 0.999)
    a0 = a_c ** (1.0 / LRU_C)
    lru_lambda = jnp.log(a0) - jnp.log1p(-a0)
    w_rec_o = nrm(ks[13], (DEPTH, D_REC, D_MODEL), D_REC)
    w_out = nrm(ks[14], (DEPTH, D_MODEL, D_MODEL), D_MODEL)
    ln2_g = 1.0 + 0.05 * jax.random.normal(ks[15], (DEPTH, D_MODEL), f32)
    w_ff1 = nrm(ks[16], (DEPTH, D_MODEL, D_FF), D_MODEL)
    w_ff2 = nrm(ks[17], (DEPTH, D_FF, D_MODEL), D_FF)
    lnf_g = 1.0 + 0.05 * jax.random.normal(ks[18], (D_MODEL,), f32)
    return {"x": x, "ln1_g": ln1_g, "w_in": w_in, "b_in": b_in, "rpb": rpb,
            "w_att_o": w_att_o, "conv_w": conv_w, "conv_b": conv_b,
            "w_rg_a": w_rg_a, "b_rg_a": b_rg_a, "w_rg_i": w_rg_i, "b_rg_i": b_rg_i,
            "lru_lambda": lru_lambda, "w_rec_o": w_rec_o, "w_out": w_out,
            "ln2_g": ln2_g, "w_ff1": w_ff1, "w_ff2": w_ff2, "lnf_g": lnf_g}


def reference(x, ln1_g, w_in, b_in, rpb, w_att_o, conv_w, conv_b, w_rg_a, b_rg_a,
              w_rg_i, b_rg_i, lru_lambda, w_rec_o, w_out, ln2_g, w_ff1, w_ff2, lnf_g):
    b, s, _ = x.shape
    for l in range(DEPTH):
        h = rms_norm(x, ln1_g[l])
        z = h @ w_in[l] + b_in[l]
        q, k, v, u, y_branch, g_att, g_rec = jnp.split(z, SPLITS, axis=-1)

        q = q.reshape(b, s, N_ATT_HEADS, ATT_HEAD_DIM)
        k = k.reshape(b, s, N_ATT_HEADS, ATT_HEAD_DIM)
        v = v.reshape(b, s, N_ATT_HEADS, ATT_HEAD_DIM)
        y_att = neighbourhood_attention(q, k, v, rpb[l]) @ w_att_o[l]

        u = centred_depthwise_conv(u, conv_w[l], conv_b[l])
        h_fwd = rg_lru(u, w_rg_a[l, 0], b_rg_a[l, 0], w_rg_i[l, 0], b_rg_i[l, 0], lru_lambda[l, 0], False)
        h_bwd = rg_lru(u, w_rg_a[l, 1], b_rg_a[l, 1], w_rg_i[l, 1], b_rg_i[l, 1], lru_lambda[l, 1], True)
        y_rec = ((h_fwd + h_bwd) * jax.nn.gelu(y_branch)) @ w_rec_o[l]

        mixed = jax.nn.sigmoid(g_att) * y_att + jax.nn.sigmoid(g_rec) * y_rec
        x = x + mixed @ w_out[l]

        h2 = rms_norm(x, ln2_g[l])
        x = x + jnp.square(jax.nn.relu(h2 @ w_ff1[l])) @ w_ff2[l]
    return rms_norm(x, lnf_g)
```

```python
import numpy as np
from contextlib import ExitStack

import concourse.bass as bass
import concourse.mybir as mybir
from concourse.bass_utils import run_bass_kernel_spmd
from concourse.ap import AP

F32 = mybir.dt.float32
BF16 = mybir.dt.bfloat16
AF = mybir.ActivationFunctionType
ALU = mybir.AluOpType

T = 2048
D = 1024
NCH = 8
TB = 512
NTB = T // TB
D_IN = 5632
EPS = 1e-6
GRID_W = 64
ROWS = T // GRID_W
NEG = -30000.0

P_BIN = 0
P_G1 = 44
P_G2 = 52
P_GF = 60
P_CW = 68
P_CB = 100
P_BA = 108
P_BI = 124
P_LAM = 140
NPAR = 156

Z_Q, Z_K, Z_V, Z_U, Z_Y, Z_GA, Z_GR = 0, 512, 1024, 1536, 2560, 3584, 4608

KIB = 1024
ARENA_BYTES = 207 * KIB


class Tile:
    __slots__ = ("ap", "lo", "hi", "wr", "rd")

    def __init__(self, ap, lo, hi):
        self.ap = ap
        self.lo = lo
        self.hi = hi
        self.wr = {}
        self.rd = {}


class DmaSem:
    def __init__(self, sem):
        self.sem = sem
        self.count = 0


class Sched:
    ENG = ("pe", "act", "dve", "pool", "sp")

    def __init__(self, nc, es):
        self.nc = nc
        self.es = es
        self.sem = {n: es.enter_context(nc.semaphore("s_" + n)) for n in self.ENG}
        self.count = {n: 0 for n in self.ENG}
        self.ops = {n: [] for n in self.ENG}
        self.tiles = []
        self.arena = None
        self.n_dsem = 0
        self.final_waits = []

    def set_arena(self, arena):
        self.arena = arena

    def alloc(self, lo, nbytes, dtype=F32, parts=128):
        assert lo % 4 == 0 and nbytes % 4 == 0
        hi = lo + nbytes
        assert hi <= ARENA_BYTES, (lo, nbytes)
        ap = self.arena[0:parts, lo // 4:hi // 4]
        if dtype == BF16:
            ap = ap.bitcast(BF16)
        t = Tile(ap, lo, hi)
        for o in self.tiles:
            if o.lo < hi and lo < o.hi:
                for src in (o.wr, o.rd):
                    for s, v in src.items():
                        if t.wr.get(s, (None, 0))[1] < v[1]:
                            t.wr[s] = v
        self.tiles.append(t)
        return t

    def extern_tile(self, ap):
        return Tile(ap, -1, -1)

    def dma_sem(self, name=None):
        self.n_dsem += 1
        s = self.es.enter_context(self.nc.semaphore(name or ("d%d" % self.n_dsem)))
        return DmaSem(s)

    def _deps(self, eng, reads, writes):
        own = id(self.sem[eng])
        deps = {}

        def add(k, v):
            if deps.get(k, (None, 0))[1] < v[1]:
                deps[k] = v

        for t in reads:
            for k, v in t.wr.items():
                add(k, v)
        for t in writes:
            for k, v in t.wr.items():
                add(k, v)
            for k, v in t.rd.items():
                add(k, v)
        if eng == "pe":
            deps.pop(own, None)
        return list(deps.values())

    def op(self, eng, fn, reads=(), writes=()):
        deps = self._deps(eng, reads, writes)
        self.count[eng] += 1
        c = self.count[eng]
        sem = self.sem[eng]
        self.ops[eng].append((fn, deps, (sem, 1)))
        k = id(sem)
        for t in reads:
            t.rd[k] = (sem, c)
        for t in writes:
            t.wr[k] = (sem, c)

    def dma(self, queue, dsem, pairs, reads=(), writes=(), extra_deps=()):
        deps = self._deps(queue, reads, writes) + list(extra_deps)
        n = len(pairs)
        dsem.count += 16 * n
        c = dsem.count
        sem = dsem.sem

        def fn(e, pairs=pairs, sem=sem):
            for o, i in pairs:
                e.dma_start(out=o, in_=i).then_inc(sem, 16)
            return None

        self.ops[queue].append((fn, deps, None))
        k = id(sem)
        for t in reads:
            t.rd[k] = (sem, c)
        for t in writes:
            t.wr[k] = (sem, c)

    def emit(self, eng, e):
        seen = {}
        for fn, deps, sig in self.ops[eng]:
            for sem, v in deps:
                if seen.get(id(sem), 0) < v:
                    e.wait_ge(sem, v)
                    seen[id(sem)] = v
            ins = fn(e)
            if sig is not None:
                ins.then_inc(sig[0], sig[1])
        if eng == "sp":
            for sem, v in self.final_waits:
                e.wait_ge(sem, v)


class _Stop(Exception):
    pass


def build_program(stop=None):
    nc = bass.Bass("TRN2", target_bir_lowering=False)
    es = ExitStack()
    with es:
        es.enter_context(nc.allow_low_precision("bf16 matmul operands, fp32 accumulation"))
        xT = nc.dram_tensor("xT", [D, T], F32, kind="ExternalInput").ap()
        w_in = nc.dram_tensor("w_in", [D, D_IN], F32, kind="ExternalInput").ap()
        w_att_o = nc.dram_tensor("w_att_o", [512, D], F32, kind="ExternalInput").ap()
        w_rec_o = nc.dram_tensor("w_rec_o", [D, D], F32, kind="ExternalInput").ap()
        w_out = nc.dram_tensor("w_out", [D, D], F32, kind="ExternalInput").ap()
        w_ff1 = nc.dram_tensor("w_ff1", [D, 4 * D], F32, kind="ExternalInput").ap()
        w_ff2 = nc.dram_tensor("w_ff2", [4 * D, D], F32, kind="ExternalInput").ap()
        w_rg = nc.dram_tensor("w_rg", [4, 16, 64, 64], F32, kind="ExternalInput").ap()
        params = nc.dram_tensor("params", [128, NPAR], F32, kind="ExternalInput").ap()
        t2d = nc.dram_tensor("t2", [4, 128, 2 * 14 * 64], F32, kind="ExternalInput").ap()
        identd = nc.dram_tensor("ident", [128, 128], F32, kind="ExternalInput").ap()
        outT = nc.dram_tensor("outT", [D, T], F32, kind="ExternalOutput").ap()

        arena = es.enter_context(nc.sbuf_tensor("arena", [128, ARENA_BYTES // 4], F32))
        S = Sched(nc, es)
        S.set_arena(arena)
        psum = []
        psall = es.enter_context(nc.psum_tensor("psall", [128, 8 * TB], F32))
        for i in range(8):
            psum.append(Tile(psall[:, i * TB:(i + 1) * TB], -1, -1))

        def pair_ap(i):
            return psall[:, 2 * i * TB:(2 * i + 2) * TB]
        ps_rr = [0]

        ps_excl = set()

        def next_ps():
            while True:
                i = ps_rr[0] % 8
                ps_rr[0] += 1
                if i not in ps_excl:
                    return psum[i]

        def next_pair():
            while True:
                i = ps_rr[0] % 8
                if i % 2:
                    ps_rr[0] += 1
                    continue
                ps_rr[0] += 2
                if i not in ps_excl and (i + 1) not in ps_excl:
                    return psum[i], psum[i + 1], pair_ap(i // 2)

        off = 0
        PAR = S.alloc(off, NPAR * 4); off += NPAR * 4
        ONES = S.alloc(off, 128 * 2, BF16); off += 256
        CL = S.alloc(off, 16 * 4); off += 64
        CL2 = S.alloc(off, 16 * 4); off += 64
        LTMP = S.alloc(off, 16 * 4); off += 64
        off = 2 * KIB
        NSUB = 16
        SUB = 2 * KIB
        wsub = [S.alloc(off + i * SUB, SUB, BF16) for i in range(NSUB)]
        wsems = [S.dma_sem("wsem%d" % i) for i in range(NSUB)]
        w_ptr = [0]
        w_hist = []
        W_DEPTH = 4

        sp_hist = []
        SP_DEPTH = 4

        def sp_dma(dsem, pairs, reads=(), writes=()):
            extra = [sp_hist[-SP_DEPTH]] if len(sp_hist) >= SP_DEPTH else []
            S.dma("sp", dsem, pairs, reads=reads, writes=writes, extra_deps=extra)
            sp_hist.append((dsem.sem, dsem.count))

        def pool_dma(dsem, pairs, writes):
            extra = [w_hist[-W_DEPTH]] if len(w_hist) >= W_DEPTH else []
            S.dma("pool", dsem, pairs, writes=writes, extra_deps=extra)
            w_hist.append((dsem.sem, dsem.count))
        off += NSUB * SUB
        HT_OFF = off
        off += 32 * KIB
        REC_OFF = off
        off += 32 * KIB
        ATT_OFF = off
        off += 16 * KIB
        SCR = off
        assert SCR == 114 * KIB

        def par(col, n=1):
            return PAR.ap[:, col:col + n]

        def wload(w2d, r0, nr, c0, ncols):
            nk = nr // 128
            nbytes = nk * ncols * 2
            k = 1 if nbytes <= SUB else (2 if nbytes <= 2 * SUB else 4)
            assert nbytes <= k * SUB
            p = w_ptr[0]
            if p % k:
                p += k - p % k
            i0 = p % NSUB
            w_ptr[0] = p + k
            tiles = wsub[i0:i0 + k]
            lo = wsub[i0].lo
            ap = S.arena[:, lo // 4:(lo + k * SUB) // 4].bitcast(BF16)
            view = ap[:, 0:nk * ncols].rearrange("p (k c) -> p k c", k=nk)
            src = w2d[r0:r0 + nr, c0:c0 + ncols].rearrange("(k p) c -> p k c", p=128)
            pool_dma(wsems[i0], [(view, src)], tiles)
            return tiles, view

        def mm_group(out_tile, out_ap, terms, reads):
            def fn(e, terms=terms, out_ap=out_ap):
                n = len(terms)
                ins = None
                for i, (l, r) in enumerate(terms):
                    ins = e.matmul(out_ap, l, r, start=(i == 0), stop=(i == n - 1))
                return ins
            S.op("pe", fn, reads=reads, writes=[out_tile])

        csem = S.dma_sem("csem")
        S.dma("sp", csem, [(PAR.ap[:, :], params[:, :])], writes=[PAR])
        S.op("dve", lambda e: e.memset(ONES.ap[:, :], 1.0), writes=[ONES])
        EPSC = S.alloc(1600, 4)
        ONEC = S.alloc(1604, 4)
        S.op("dve", lambda e: e.memset(EPSC.ap[:, :], EPS), writes=[EPSC])
        S.op("dve", lambda e: e.memset(ONEC.ap[:, :], 1.0), writes=[ONEC])
        S.op("act", lambda e: e.activation(out=LTMP.ap[:, :], in_=par(P_LAM, 16), func=AF.Exp, scale=-1.0),
             reads=[PAR], writes=[LTMP])
        S.op("act", lambda e: e.activation(out=LTMP.ap[:, :], in_=LTMP.ap[:, :], func=AF.Ln, bias=ONEC.ap[:, 0:1]),
             reads=[LTMP, ONEC], writes=[LTMP])
        S.op("dve", lambda e: e.tensor_scalar(out=CL.ap[:, :], in0=LTMP.ap[:, :], scalar1=-8.0, scalar2=None,
                                              op0=ALU.mult), reads=[LTMP], writes=[CL])
        S.op("dve", lambda e: e.tensor_scalar(out=CL2.ap[:, :], in0=LTMP.ap[:, :], scalar1=-16.0, scalar2=None,
                                              op0=ALU.mult), reads=[LTMP], writes=[CL2])
        IDENT = S.alloc(1100, 256, BF16)
        HB = S.alloc(1360, 128)
        CLH = S.alloc(1488, 64)
        QTR = S.alloc(1552, 4)
        isem = S.dma_sem("isem")
        pool_dma(isem, [(IDENT.ap[:, :], identd[:, :])], [IDENT])
        S.op("dve", lambda e: e.memset(QTR.ap[:, :], 0.25), writes=[QTR])
        S.op("dve", lambda e: e.tensor_scalar(out=HB.ap[:, :], in0=par(P_BA, 32), scalar1=0.5, scalar2=None,
                                              op0=ALU.mult), reads=[PAR], writes=[HB])
        S.op("dve", lambda e: e.tensor_scalar(out=CLH.ap[:, :], in0=LTMP.ap[:, :], scalar1=-4.0, scalar2=None,
                                              op0=ALU.mult), reads=[LTMP], writes=[CLH])

        xl_sems = [S.dma_sem("xl%d" % c) for c in range(NCH)]

        def stats_chunk(xt, sq, banks, first, last, eng="act"):
            if eng == "act":
                S.op("act", lambda e: e.activation(out=sq.ap[:, :], in_=xt.ap[:, :], func=AF.Square),
                     reads=[xt], writes=[sq])
            else:
                S.op("dve", lambda e: e.tensor_tensor(out=sq.ap[:, :], in0=xt.ap[:, :], in1=xt.ap[:, :], op=ALU.mult),
                     reads=[xt], writes=[sq])
            for tb in range(NTB):
                def fn(e, tb=tb):
                    return e.matmul(banks[tb].ap[:, :], ONES.ap[:, :], sq.ap[:, tb * TB:(tb + 1) * TB],
                                    start=first, stop=last)
                S.op("pe", fn, reads=[ONES, sq], writes=[banks[tb]])

        def stats_finish(RSTD, banks):
            for tb in range(NTB):
                sl = slice(tb * TB, (tb + 1) * TB)
                S.op("act", lambda e, tb=tb, sl=sl: e.activation(out=RSTD.ap[:, sl], in_=banks[tb].ap[:, :],
                                                                  func=AF.Ln, scale=1.0 / D, bias=EPSC.ap[:, 0:1]),
                     reads=[banks[tb], EPSC], writes=[RSTD])
            S.op("act", lambda e: e.activation(out=RSTD.ap[:, :], in_=RSTD.ap[:, :], func=AF.Exp, scale=-0.5),
                 reads=[RSTD], writes=[RSTD])

        def stats_acc(xt, sqf, ssq, first):
            S.op("act", lambda e: e.activation(out=sqf.ap[:, :], in_=xt.ap[:, :], func=AF.Square),
                 reads=[xt], writes=[sqf])
            if first:
                S.op("dve", lambda e: e.tensor_copy(out=ssq.ap[:, :], in_=sqf.ap[:, :]), reads=[sqf], writes=[ssq])
            else:
                S.op("dve", lambda e: e.tensor_tensor(out=ssq.ap[:, :], in0=ssq.ap[:, :], in1=sqf.ap[:, :], op=ALU.add),
                     reads=[ssq, sqf], writes=[ssq])

        def stats_acc_finish(ssq, ssb, RSTD):
            S.op("dve", lambda e: e.tensor_copy(out=ssb.ap[:, :], in_=ssq.ap[:, :]), reads=[ssq], writes=[ssb])
            banks = [next_ps() for _ in range(NTB)]
            for tb in range(NTB):
                def fn(e, tb=tb):
                    return e.matmul(banks[tb].ap[:, :], ONES.ap[:, :], ssb.ap[:, tb * TB:(tb + 1) * TB],
                                    start=True, stop=True)
                S.op("pe", fn, reads=[ONES, ssb], writes=[banks[tb]])
            stats_finish(RSTD, banks)

        def rms_stats(XT, sq_tiles, RSTD):
            banks = [next_ps() for _ in range(NTB)]
            for c in range(NCH):
                stats_chunk(XT[c], sq_tiles[c % len(sq_tiles)], banks, c == 0, c == NCH - 1,
                            eng=("act" if c % 2 == 0 else "dve"))
            stats_finish(RSTD, banks)


        dsem_dbg = S.dma_sem("dbg")

        def checkpoint(k, tiles):
            if stop != k:
                return
            for i, t in enumerate(tiles):
                S.dma("pool", dsem_dbg, [(outT[i * 128:(i + 1) * 128, :], t.ap[:, 0:T])], reads=[t])
            S.final_waits = [(dsem_dbg.sem, dsem_dbg.count)]
            raise _Stop()

        try:
            HT = [S.alloc(HT_OFF + c * 4 * KIB, 4 * KIB, BF16) for c in range(NCH)]
            XT0 = [S.alloc(SCR + c * 8 * KIB, 8 * KIB) for c in range(NCH)]
            SQ0 = [S.alloc(SCR + 64 * KIB + i * 4 * KIB, 4 * KIB, BF16) for i in range(2)]
            RSTD0 = S.alloc(SCR + 72 * KIB, 8 * KIB)
            for c in range(NCH):
                sp_dma(xl_sems[c], [(XT0[c].ap[:, :], xT[c * 128:(c + 1) * 128, :])], writes=[XT0[c]])
            rms_stats(XT0, SQ0, RSTD0)
            HTS = [[Tile(HT[c].ap[:, tb * TB:(tb + 1) * TB], -1, -1) for tb in range(NTB)] for c in range(NCH)]
            for tb in range(NTB):
                for c in range(NCH):
                    sl = slice(tb * TB, (tb + 1) * TB)
                    S.op("dve", lambda e, c=c, sl=sl: e.scalar_tensor_tensor(
                        out=HT[c].ap[:, sl], in0=XT0[c].ap[:, sl], scalar=par(P_G1 + c), in1=RSTD0.ap[:, sl],
                        op0=ALU.mult, op1=ALU.mult), reads=[XT0[c], PAR, RSTD0], writes=[HT[c], HTS[c][tb]])

            def proj_fm(wview, ncol_lo, k_tiles, nk, tb):
                sl = slice(tb * TB, (tb + 1) * TB)
                return [(wview[:, k, ncol_lo:ncol_lo + 128], k_tiles[k].ap[:, sl]) for k in range(nk)]

            checkpoint(0, HT)
            RECT = [S.alloc(REC_OFF + c * 4 * KIB, 4 * KIB, BF16) for c in range(NCH)]
            WG = S.alloc(ATT_OFF, 8 * KIB, BF16)
            wgv = WG.ap[:, :].rearrange("p (g c m) -> p g c m", g=4, c=8)
            S.op("pool", lambda e: e.memset(WG.ap[:, :], 0.0), writes=[WG])
            wgsem = S.dma_sem("wgsem")
            for g in range(4):
                pairs = []
                for a in range(2):
                    src = w_rg[g, a:16:2, :, :].rearrange("c p d -> p c d")
                    dst = wgv[a * 64:(a + 1) * 64, g, :, a * 64:(a + 1) * 64]
                    pairs.append((dst, src))
                pool_dma(wgsem, pairs, [WG])
            DGC = [S.alloc(ATT_OFF + 12 * KIB + i * KIB, KIB, BF16) for i in range(2)]
            o = SCR
            UW = 2052
            UB_ = [S.alloc(o + i * UW * 2, UW * 2, BF16) for i in range(2)]; o += 2 * UW * 2
            GY_ = [S.alloc(o + i * 4 * KIB, 4 * KIB, BF16) for i in range(3)]; o += 12 * KIB
            UC_ = [S.alloc(o + i * 8 * KIB, 8 * KIB) for i in range(2)]; o += 16 * KIB
            UCB_ = [S.alloc(ATT_OFF + 8 * KIB, 4 * KIB, BF16), S.alloc(o, 4 * KIB, BF16)]; o += 4 * KIB
            NSET = 3
            HALF = T // 2
            SA = [S.alloc(o + i * 4 * KIB, 4 * KIB) for i in range(NSET)]; o += NSET * 4 * KIB
            SM = [S.alloc(o + i * 4 * KIB, 4 * KIB) for i in range(NSET)]; o += NSET * 4 * KIB
            ST = [S.alloc(o + i * 4 * KIB, 4 * KIB) for i in range(NSET)]; o += NSET * 4 * KIB
            HH = [[S.alloc(o + (d * 2 + h) * 4 * KIB, 4 * KIB) for h in range(2)] for d in range(2)]; o += 16 * KIB
            assert o <= ARENA_BYTES, o

            def rev(ap2d, n):
                pst = ap2d.ap[0][0]
                npart = ap2d.ap[0][1]
                return AP(ap2d.tensor, ap2d.offset + (n - 1), [[pst, npart], [-1, n]])

            lw = {}
            for c in range(NCH):
                lw[("u", c)] = wload(w_in, 0, D, Z_U + c * 128, 128)
                lw[("y", c)] = wload(w_in, 0, D, Z_Y + c * 128, 128)
            for i in range(2):
                S.op("pool", lambda e, i=i: e.memset(UB_[i].ap[:, :], 0.0), writes=[UB_[i]])

            def lru_s1(c):
                wu_t, wu = lw[("u", c)]
                wy_t, wy = lw[("y", c)]
                UB, GY = UB_[c % 2], GY_[c % 3]
                dg = DGC[c % 2]
                dgv = dg.ap[:, :].rearrange("p (j m) -> p j m", j=4)
                for j in range(4):
                    S.op("dve", lambda e, c=c, j=j, dgv=dgv: e.tensor_scalar(
                        out=dgv[:, j, :], in0=IDENT.ap[:, :], scalar1=par(P_CW + j * 8 + c), scalar2=None,
                        op0=ALU.mult), reads=[IDENT, PAR], writes=[dg])
                for hb in range(2):
                    pa, pb, pap = next_pair()
                    for tl, ps in ((0, pa), (1, pb)):
                        mm_group(ps, ps.ap[:, :], proj_fm(wu, 0, HT, NCH, 2 * hb + tl), reads=(([HTS[k][2 * hb + tl] for k in range(NCH)] if c == 0 else HT) + wu_t))
                    S.op("dve", lambda e, pap=pap, hb=hb, UB=UB, c=c: e.tensor_scalar(
                        out=UB.ap[:, 2 + hb * 2 * TB:2 + (hb + 1) * 2 * TB], in0=pap,
                        scalar1=par(P_BIN + Z_U // 128 + c), scalar2=None, op0=ALU.add),
                        reads=[pa, pb, PAR], writes=[UB])
                for hb in range(2):
                    pa, pb, pap = next_pair()
                    for tl, ps in ((0, pa), (1, pb)):
                        mm_group(ps, ps.ap[:, :], proj_fm(wy, 0, HT, NCH, 2 * hb + tl), reads=(([HTS[k][2 * hb + tl] for k in range(NCH)] if c == 0 else HT) + wy_t))
                    S.op("act", lambda e, pap=pap, hb=hb, GY=GY, c=c: e.activation(
                        out=GY.ap[:, hb * 2 * TB:(hb + 1) * 2 * TB], in_=pap, func=AF.Gelu_apprx_tanh,
                        bias=par(P_BIN + Z_Y // 128 + c)),
                        reads=[pa, pb, PAR], writes=[GY])

            def lru_s2(c):
                UB, UC, dg, UCB = UB_[c % 2], UC_[c % 2], DGC[c % 2], UCB_[c % 2]
                dgv = dg.ap[:, :].rearrange("p (j m) -> p j m", j=4)
                for hb in range(2):
                    pa, pb, pap = next_pair()
                    for tl, ps in ((0, pa), (1, pb)):
                        tb = 2 * hb + tl
                        terms = [(dgv[:, j, :], UB.ap[:, tb * TB + j:tb * TB + j + TB]) for j in range(4)]
                        mm_group(ps, ps.ap[:, :], terms, reads=[dg, UB])
                    S.op("dve", lambda e, pap=pap, hb=hb, UC=UC, c=c: e.tensor_scalar(
                        out=UC.ap[:, hb * 2 * TB:(hb + 1) * 2 * TB], in0=pap, scalar1=par(P_CB + c), scalar2=None,
                        op0=ALU.add), reads=[pa, pb, PAR], writes=[UC])
                S.op("dve", lambda e, UC=UC: e.tensor_copy(out=UCB.ap[:, :], in_=UC.ap[:, :]),
                     reads=[UC], writes=[UCB])

            ITEMS = [(0, 0), (1, 1), (0, 1), (1, 0)]
            item_ctr = [0]

            cur = {}

            def lru_s3(c, p):
                UCB = UCB_[c % 2]
                its = []
                for (d, h) in ITEMS[2 * p:2 * p + 2]:
                    s_ = item_ctr[0] % NSET
                    item_ctr[0] += 1
                    its.append((d, h, SA[s_], SM[s_], ST[s_]))
                cur[(c, p)] = its
                for (d, h, A, M, TI) in its:
                    for gi, G in ((0, A), (1, TI)):
                        g = d * 2 + gi
                        hbcol = gi * 16 + d * 8 + c
                        pa, pb, pap = next_pair()
                        for tl, ps in ((0, pa), (1, pb)):
                            tb = 2 * h + tl
                            mm_group(ps, ps.ap[:, :], [(wgv[:, g, c, :], UCB.ap[:, tb * TB:(tb + 1) * TB])], reads=[WG, UCB])
                        S.op("act", lambda e, pap=pap, G=G, hbcol=hbcol: e.activation(
                            out=G.ap[:, :], in_=pap, func=AF.Tanh, scale=0.5,
                            bias=HB.ap[:, hbcol:hbcol + 1]), reads=[pa, pb, HB], writes=[G])

            def lru_s4(c, p):
                its = cur[(c, p)]
                for (d, h, A, M, TI) in its:
                    ccol = d * 8 + c
                    S.op("act", lambda e, A=A, M=M, ccol=ccol: e.activation(
                        out=M.ap[:, :], in_=A.ap[:, :], func=AF.Exp,
                        scale=CL.ap[:, ccol:ccol + 1], bias=CL.ap[:, ccol:ccol + 1]),
                        reads=[A, CL], writes=[M])
                    S.op("act", lambda e, A=A, ccol=ccol: e.activation(
                        out=A.ap[:, :], in_=A.ap[:, :], func=AF.Exp,
                        scale=CLH.ap[:, ccol:ccol + 1], bias=CLH.ap[:, ccol:ccol + 1]),
                        reads=[A, CLH], writes=[A])
                for (d, h, A, M, TI) in its:
                    S.op("act", lambda e, M=M: e.activation(out=M.ap[:, :], in_=M.ap[:, :], func=AF.Sqrt,
                                                            scale=-0.25, bias=QTR.ap[:, 0:1]),
                         reads=[M, QTR], writes=[M])

            def lru_s5(c, p):
                UC = UC_[c % 2]
                its = cur[(c, p)]
                for (d, h, A, M, TI) in its:
                    hs = slice(h * HALF, (h + 1) * HALF)
                    S.op("dve", lambda e, TI=TI, UC=UC, hs=hs: e.scalar_tensor_tensor(
                        out=TI.ap[:, :], in0=TI.ap[:, :], scalar=1.0, in1=UC.ap[:, hs], op0=ALU.add, op1=ALU.mult),
                        reads=[TI, UC], writes=[TI])
                    S.op("dve", lambda e, TI=TI, M=M: e.tensor_tensor(out=TI.ap[:, :], in0=TI.ap[:, :], in1=M.ap[:, :],
                                                                      op=ALU.mult), reads=[TI, M], writes=[TI])
                    Hout = HH[d][h]
                    if d == 0:
                        if h == 0:
                            S.op("dve", lambda e, A=A, TI=TI, Hout=Hout: e.tensor_tensor_scan(
                                out=Hout.ap[:, :], data0=A.ap[:, :], data1=TI.ap[:, :], initial=0.0,
                                op0=ALU.mult, op1=ALU.add), reads=[A, TI], writes=[Hout])
                        else:
                            prev = HH[0][0]
                            S.op("dve", lambda e, A=A, TI=TI, Hout=Hout, prev=prev: e.tensor_tensor_scan(
                                out=Hout.ap[:, :], data0=A.ap[:, :], data1=TI.ap[:, :],
                                initial=prev.ap[:, HALF - 1:HALF], op0=ALU.mult, op1=ALU.add),
                                reads=[A, TI, prev], writes=[Hout])
                    else:
                        if h == 1:
                            S.op("dve", lambda e, A=A, TI=TI, Hout=Hout: e.tensor_tensor_scan(
                                out=rev(Hout.ap[:, :], HALF), data0=rev(A.ap[:, :], HALF), data1=rev(TI.ap[:, :], HALF),
                                initial=0.0, op0=ALU.mult, op1=ALU.add), reads=[A, TI], writes=[Hout])
                        else:
                            prev = HH[1][1]
                            S.op("dve", lambda e, A=A, TI=TI, Hout=Hout, prev=prev: e.tensor_tensor_scan(
                                out=rev(Hout.ap[:, :], HALF), data0=rev(A.ap[:, :], HALF), data1=rev(TI.ap[:, :], HALF),
                                initial=prev.ap[:, 0:1], op0=ALU.mult, op1=ALU.add),
                                reads=[A, TI, prev], writes=[Hout])

            def lru_s6(c, h):
                GY = GY_[c % 3]
                hs = slice(h * HALF, (h + 1) * HALF)
                S.op("dve", lambda e, h=h: e.tensor_tensor(out=HH[0][h].ap[:, :], in0=HH[0][h].ap[:, :],
                                                         in1=HH[1][h].ap[:, :], op=ALU.add),
                     reads=[HH[0][h], HH[1][h]], writes=[HH[0][h]])
                S.op("dve", lambda e, c=c, h=h, GY=GY, hs=hs: e.tensor_tensor(
                    out=RECT[c].ap[:, hs], in0=HH[0][h].ap[:, :], in1=GY.ap[:, hs], op=ALU.mult),
                    reads=[HH[0][h], GY], writes=[RECT[c]])

            lru_s1(0)
            lru_s2(0)
            for c in range(NCH):
                lru_s3(c, 0)
                lru_s4(c, 0)
                if c + 1 < NCH:
                    lru_s1(c + 1)
                lru_s5(c, 0)
                lru_s3(c, 1)
                if c + 1 < NCH:
                    lru_s2(c + 1)
                lru_s4(c, 1)
                lru_s5(c, 1)
                lru_s6(c, 1)
                lru_s6(c, 0)

            checkpoint(1, RECT)
            ATT = [S.alloc(ATT_OFF + j * 4 * KIB, 4 * KIB, BF16) for j in range(4)]
            o = SCR
            QT = [S.alloc(o + j * 4 * KIB, 4 * KIB, BF16) for j in range(4)]; o += 16 * KIB
            KT = [S.alloc(o + j * 4 * KIB, 4 * KIB, BF16) for j in range(4)]; o += 16 * KIB
            V1 = S.alloc(o, 16 * KIB, BF16); o += 16 * KIB
            V2 = S.alloc(o, 15 * KIB, BF16); o += 15 * KIB
            T2 = [S.alloc(o + i * 3584, 3584, BF16) for i in range(2)]; o += 7 * KIB
            PT = [S.alloc(o + i * 2 * KIB, 2 * KIB, BF16) for i in range(2)]; o += 4 * KIB
            NRM = [S.alloc(o + i * KIB, KIB) for i in range(2)]; o += 2 * KIB
            RDN = [S.alloc(o + i * KIB, KIB) for i in range(2)]; o += 2 * KIB
            PS1 = [S.alloc(o + i * KIB, KIB, BF16) for i in range(2)]; o += 2 * KIB
            PS2 = [S.alloc(o + i * 512, 512, BF16) for i in range(2)]; o += KIB
            assert o <= ARENA_BYTES
            t2sems = [S.dma_sem("t2s%d" % i) for i in range(2)]
            v1v = V1.ap[:, :].rearrange("p (b f) -> p b f", b=16)
            v2v = V2.ap[:, :].rearrange("p (b f) -> p b f", b=15)

            wv_t, wv = wload(w_in, 0, D, Z_V, 512)
            for (vv, Vt, nb, toff) in ((v1v, V1, 16, 0),):
                for b in range(nb):
                    t0 = toff + b * 128
                    ps = next_ps()
                    terms = [(HT[k].ap[:, t0:t0 + 128], wv[:, k, :]) for k in range(NCH)]
                    mm_group(ps, ps.ap[:, :], terms, reads=HT + wv_t)
                    if b % 2 == 0:
                        S.op("act", lambda e, ps=ps, vv=vv, b=b: e.activation(out=vv[:, b, :], in_=ps.ap[:, :], func=AF.Copy),
                             reads=[ps], writes=[Vt])
                    else:
                        S.op("dve", lambda e, ps=ps, vv=vv, b=b: e.tensor_copy(out=vv[:, b, :], in_=ps.ap[:, :]),
                             reads=[ps], writes=[Vt])

            v2sem = S.dma_sem("v2sem")
            sp_dma(v2sem, [(v2v[0:64, 0:15, :], v1v[64:128, 0:15, :]),
                           (v2v[64:128, 0:15, :], v1v[0:64, 1:16, :])], reads=[V1], writes=[V2])

            for j in range(4):
                wq_t, wq = wload(w_in, 0, D, Z_Q + j * 128, 128)
                wk_t, wk = wload(w_in, 0, D, Z_K + j * 128, 128)
                for tb in range(NTB):
                    sl = slice(tb * TB, (tb + 1) * TB)
                    ps = next_ps()
                    mm_group(ps, ps.ap[:, :], proj_fm(wq, 0, HT, NCH, tb), reads=HT + wq_t)
                    S.op("dve", lambda e, ps=ps, sl=sl, j=j: e.tensor_scalar(
                        out=QT[j].ap[:, sl], in0=ps.ap[:, :], scalar1=par(P_BIN + Z_Q // 128 + j), scalar2=0.125,
                        op0=ALU.add, op1=ALU.mult), reads=[ps, PAR], writes=[QT[j]])
                    ps = next_ps()
                    mm_group(ps, ps.ap[:, :], proj_fm(wk, 0, HT, NCH, tb), reads=HT + wk_t)
                    S.op("act", lambda e, ps=ps, sl=sl, j=j: e.activation(
                        out=KT[j].ap[:, sl], in_=ps.ap[:, :], func=AF.Identity, bias=par(P_BIN + Z_K // 128 + j)),
                        reads=[ps, PAR], writes=[KT[j]])

            units = [(j, qg, qp) for j in range(4) for qg in range(ROWS // 4) for qp in range(2)]
            ust = {}
            t2_state = {}

            def rec_S(ui):
                j, qg, qp = units[ui]
                if j not in t2_state:
                    t2t = T2[j % 2]
                    pool_dma(t2sems[j % 2], [(t2t.ap[:, :], t2d[j, :, :])], [t2t])
                    t2_state[j] = t2t
                qrs = [qg * 4 + qp * 2 + r for r in range(2)]
                rss = [min(max(qr - 4, 0), ROWS - 8) for qr in qrs]
                e0s = [rs - qr + 7 for rs, qr in zip(rss, qrs)]
                sbank = [psum[(2 * ui) % 4], psum[(2 * ui + 1) % 4]]
                ust[ui] = dict(qrs=qrs, rss=rss, e0s=e0s, sbank=sbank, pt=PT[ui % 2])
                t2t = t2_state[j]
                t2v = t2t.ap[:, :].rearrange("p (h e q) -> p h e q", h=2, e=14)

                def fn_s(e, j=j, qrs=qrs, rss=rss, sbank=sbank, e0s=e0s, t2v=t2v):
                    ins = None
                    for r in range(2):
                        for hh in range(2):
                            ins = e.matmul(sbank[hh].ap[:, r * 256:(r + 1) * 256], IDENT.ap[:, :],
                                           t2v[:, hh, e0s[r]:e0s[r] + 7:2, :], start=True, stop=False)
                        for m in range(4):
                            for hh in range(2):
                                pl = slice(hh * 64, (hh + 1) * 64)
                                k0 = (rss[r] + 2 * m) * 64
                                col = (r * 4 + m) * 64
                                ins = e.matmul(sbank[hh].ap[:, col:col + 64], KT[j].ap[pl, k0:k0 + 128],
                                               QT[j].ap[pl, qrs[r] * 64:(qrs[r] + 1) * 64],
                                               start=False, stop=(m == 3))
                    return ins
                S.op("pe", fn_s, reads=[KT[j], QT[j], t2t, IDENT], writes=sbank)

            def rec_mid(ui):
                st = ust[ui]
                sbank, pt = st["sbank"], st["pt"]
                pap = pair_ap(ui % 2)
                S.op("act", lambda e, pap=pap, pt=pt: e.activation(out=pt.ap[:, :], in_=pap, func=AF.Exp),
                     reads=list(sbank), writes=[pt])

            nd_state = {}

            def rec_PV(ui):
                j, qg, qp = units[ui]
                st = ust[ui]
                rss, pt = st["rss"], st["pt"]
                if (j, qg) not in nd_state:
                    nd_state[(j, qg)] = psum[4 + (j * (ROWS // 4) + qg) % 3]
                nd = nd_state[(j, qg)]

                ps1, ps2 = PS1[ui % 2], PS2[ui % 2]
                pt4 = pt.ap[:, :].rearrange("p (a m q) -> p a m q", a=4, m=4)
                ps1v = ps1.ap[:, :].rearrange("p (a t q) -> p a t q", a=4, t=2)
                S.op("dve", lambda e, pt4=pt4, ps1v=ps1v: e.tensor_tensor(
                    out=ps1v, in0=pt4[:, :, 0:4:2, :], in1=pt4[:, :, 1:4:2, :], op=ALU.add),
                    reads=[pt], writes=[ps1])
                S.op("dve", lambda e, ps1v=ps1v, ps2=ps2: e.tensor_tensor(
                    out=ps2.ap[:, :].rearrange("p (a q) -> p a q", a=4), in0=ps1v[:, :, 0, :], in1=ps1v[:, :, 1, :],
                    op=ALU.add), reads=[ps1], writes=[ps2])

                def fn_pv(e, j=j, qp=qp, rss=rss, pt=pt, nd=nd, ps2=ps2):
                    ins = None
                    ptv = pt.ap[:, :].rearrange("p (h r m q) -> p h r m q", h=2, r=2, m=4)
                    for r in range(2):
                        qi = qp * 2 + r
                        for m in range(4):
                            b = rss[r] + 2 * m
                            for hh in range(2):
                                pl = slice(hh * 64, (hh + 1) * 64)
                                h = 2 * j + hh
                                vsrc = v1v[:, b // 2, h * 64:(h + 1) * 64] if b % 2 == 0 else \
                                    v2v[:, (b - 1) // 2, h * 64:(h + 1) * 64]
                                ins = e.matmul(nd.ap[pl, qi * 64:(qi + 1) * 64], vsrc, ptv[:, hh, r, m, :],
                                               start=(m == 0), stop=(m == 3))
                    for hh in range(2):
                        pl = slice(hh * 64, (hh + 1) * 64)
                        ins = e.matmul(nd.ap[pl, 256 + qp * 128:256 + (qp + 1) * 128], ONES.ap[:, 0:64],
                                       ps2.ap[:, hh * 128:(hh + 1) * 128], start=True, stop=True)
                    return ins
                S.op("pe", fn_pv, reads=[pt, ps2, V1, V2, ONES], writes=[nd])
                if qp == 1:
                    nrm = NRM[qg % 2]
                    rdn = RDN[qg % 2]
                    nnum = nd.ap[:, 0:256]
                    nden = nd.ap[:, 256:512]

                    def n_act1(nd=nd, nden=nden, rdn=rdn):
                        S.op("act", lambda e: e.activation(out=rdn.ap[:, :], in_=nden, func=AF.Ln),
                             reads=[nd], writes=[rdn])
                        S.op("act", lambda e: e.activation(out=rdn.ap[:, :], in_=rdn.ap[:, :], func=AF.Exp, scale=-1.0),
                             reads=[rdn], writes=[rdn])

                    def n_dve(nd=nd, nnum=nnum, nrm=nrm, rdn=rdn):
                        S.op("dve", lambda e: e.tensor_tensor(out=nrm.ap[:, :], in0=nnum, in1=rdn.ap[:, :], op=ALU.mult),
                             reads=[nd, rdn], writes=[nrm])

                    def n_act2(nrm=nrm, j=j, qg=qg):
                        S.op("act", lambda e: e.activation(
                            out=ATT[j].ap[:, qg * 256:(qg + 1) * 256], in_=nrm.ap[:, :], func=AF.Identity,
                            bias=par(P_BIN + Z_V // 128 + j)), reads=[nrm, PAR], writes=[ATT[j]])
                    pend_a1.append((ui + 2, n_act1))
                    pend_d.append((ui + 2, n_dve))
                    pend_a2.append((ui + 3, n_act2))

            pend_a1, pend_d, pend_a2 = [], [], []

            def flush(lst, ui):
                while lst and lst[0][0] <= ui:
                    lst.pop(0)[1]()

            rec_S(0)
            for ui in range(len(units)):
                if ui + 1 < len(units):
                    rec_S(ui + 1)
                flush(pend_a2, ui)
                flush(pend_a1, ui)
                rec_mid(ui)
                flush(pend_d, ui)
                rec_PV(ui)
            for lst in (pend_a1, pend_d, pend_a2):
                flush(lst, 10 ** 9)

            checkpoint(2, ATT + QT)
            o = SCR
            MIX = [S.alloc(o + m * 4 * KIB, 4 * KIB, BF16) for m in range(NCH)]; o += 32 * KIB
            SGA = [S.alloc(o + i * 2 * KIB, 2 * KIB) for i in range(2)]; o += 4 * KIB
            SGR = [S.alloc(o + i * 2 * KIB, 2 * KIB) for i in range(2)]; o += 4 * KIB
            MXA = [S.alloc(o + i * 2 * KIB, 2 * KIB) for i in range(2)]; o += 4 * KIB
            MXR = [S.alloc(o + i * 2 * KIB, 2 * KIB) for i in range(2)]; o += 4 * KIB
            it = 0
            for mg in range(2):
                for mm in range(4):
                    m = mg * 4 + mm
                    wga_t, wga = wload(w_in, 0, D, Z_GA + m * 128, 128)
                    wao_t, wao = wload(w_att_o, 0, 512, m * 128, 128)
                    wgr_t, wgr = wload(w_in, 0, D, Z_GR + m * 128, 128)
                    wro_t, wro = wload(w_rec_o, 0, D, m * 128, 128)
                    for tb in range(NTB):
                        sl = slice(tb * TB, (tb + 1) * TB)
                        sga, sgr, mxa, mxr = SGA[it % 2], SGR[it % 2], MXA[it % 2], MXR[it % 2]
                        it += 1
                        ps = next_ps()
                        mm_group(ps, ps.ap[:, :], proj_fm(wga, 0, HT, NCH, tb), reads=HT + wga_t)
                        S.op("act", lambda e, ps=ps, sga=sga, m=m: e.activation(
                            out=sga.ap[:, :], in_=ps.ap[:, :], func=AF.Sigmoid, bias=par(P_BIN + Z_GA // 128 + m)),
                            reads=[ps, PAR], writes=[sga])
                        ps = next_ps()
                        mm_group(ps, ps.ap[:, :], proj_fm(wao, 0, ATT, 4, tb), reads=ATT + wao_t)
                        S.op("dve", lambda e, ps=ps, sga=sga, mxa=mxa: e.tensor_tensor(
                            out=mxa.ap[:, :], in0=ps.ap[:, :], in1=sga.ap[:, :], op=ALU.mult),
                            reads=[ps, sga], writes=[mxa])
                        ps = next_ps()
                        mm_group(ps, ps.ap[:, :], proj_fm(wgr, 0, HT, NCH, tb), reads=HT + wgr_t)
                        S.op("act", lambda e, ps=ps, sgr=sgr, m=m: e.activation(
                            out=sgr.ap[:, :], in_=ps.ap[:, :], func=AF.Sigmoid, bias=par(P_BIN + Z_GR // 128 + m)),
                            reads=[ps, PAR], writes=[sgr])
                        ps = next_ps()
                        mm_group(ps, ps.ap[:, :], proj_fm(wro, 0, RECT, NCH, tb), reads=RECT + wro_t)
                        S.op("dve", lambda e, ps=ps, sgr=sgr, mxr=mxr: e.tensor_tensor(
                            out=mxr.ap[:, :], in0=ps.ap[:, :], in1=sgr.ap[:, :], op=ALU.mult),
                            reads=[ps, sgr], writes=[mxr])
                        S.op("dve", lambda e, mxa=mxa, mxr=mxr, m=m, sl=sl: e.tensor_tensor(
                            out=MIX[m].ap[:, sl], in0=mxa.ap[:, :], in1=mxr.ap[:, :], op=ALU.add),
                            reads=[mxa, mxr], writes=[MIX[m]])

            checkpoint(3, MIX)
            XT = [S.alloc(HT_OFF + c * 8 * KIB, 8 * KIB) for c in range(NCH)]
            for c in range(NCH):
                sp_dma(xl_sems[c], [(XT[c].ap[:, :], xT[c * 128:(c + 1) * 128, :])], writes=[XT[c]])
            H2 = [S.alloc(SCR + 32 * KIB + c * 4 * KIB, 4 * KIB, BF16) for c in range(NCH)]
            RSTD1 = S.alloc(ATT_OFF + 8 * KIB, 8 * KIB)
            SQF1 = [S.alloc(SCR + 64 * KIB + i * 8 * KIB, 8 * KIB) for i in range(2)]
            SSQ1 = S.alloc(SCR + 80 * KIB, 8 * KIB)
            SSB1 = S.alloc(SCR + 88 * KIB, 4 * KIB, BF16)
            for mg in range(2):
                for mm in range(4):
                    m = mg * 4 + mm
                    wo_t, wo = wload(w_out, 0, D, m * 128, 128)
                    for tb in range(NTB):
                        sl = slice(tb * TB, (tb + 1) * TB)
                        ps = next_ps()
                        mm_group(ps, ps.ap[:, :], proj_fm(wo, 0, MIX, NCH, tb), reads=MIX + wo_t)
                        S.op("dve", lambda e, ps=ps, m=m, sl=sl: e.tensor_tensor(
                            out=XT[m].ap[:, sl], in0=ps.ap[:, :], in1=XT[m].ap[:, sl], op=ALU.add),
                            reads=[ps, XT[m]], writes=[XT[m]])
                    if m >= 1:
                        stats_acc(XT[m - 1], SQF1[(m - 1) % 2], SSQ1, m - 1 == 0)
                    S.op("act", lambda e, m=m: e.activation(out=H2[m].ap[:, :], in_=XT[m].ap[:, :], func=AF.Copy,
                                                            scale=par(P_G2 + m)),
                         reads=[XT[m], PAR], writes=[H2[m]])
            stats_acc(XT[NCH - 1], SQF1[(NCH - 1) % 2], SSQ1, False)
            stats_acc_finish(SSQ1, SSB1, RSTD1)

            checkpoint(4, XT)
            checkpoint(5, H2)
            RT = [S.alloc(SCR + 64 * KIB + i * 2 * KIB, 2 * KIB) for i in range(2)]
            a_offs = [SCR + i * 2 * KIB for i in range(16)]
            a_offs += [SCR + 68 * KIB + i * 2 * KIB for i in range(12)]
            a_offs += [ATT_OFF + i * 2 * KIB for i in range(4)]
            assert len(a_offs) == 32 and SCR + 68 * KIB + 24 * KIB <= ARENA_BYTES
            osems = [S.dma_sem("os%d" % i) for i in range(2)]
            AT = None
            it = 0
            for hf in range(2):
                AT = [S.alloc(a_offs[jn], 2 * KIB, BF16) for jn in range(32)]
                for jg in range(8):
                    for jj in range(4):
                        jn = jg * 4 + jj
                        w1_t, w1 = wload(w_ff1, 0, D, jn * 128, 128)
                        for tl in range(2):
                            tb = hf * 2 + tl
                            sl = slice(tb * TB, (tb + 1) * TB)
                            ps = next_ps()
                            mm_group(ps, ps.ap[:, :], proj_fm(w1, 0, H2, NCH, tb), reads=H2 + w1_t)
                            rt = RT[it % 2]
                            it += 1
                            S.op("act", lambda e, ps=ps, rt=rt: e.activation(out=rt.ap[:, :], in_=ps.ap[:, :], func=AF.Relu),
                                 reads=[ps], writes=[rt])
                            S.op("dve", lambda e, rt=rt, sl=sl: e.tensor_tensor(
                                out=rt.ap[:, :], in0=rt.ap[:, :], in1=RSTD1.ap[:, sl], op=ALU.mult),
                                reads=[rt, RSTD1], writes=[rt])
                            S.op("dve", lambda e, rt=rt, jn=jn, tl=tl, AT=AT: e.tensor_tensor(
                                out=AT[jn].ap[:, tl * TB:(tl + 1) * TB], in0=rt.ap[:, :], in1=rt.ap[:, :], op=ALU.mult),
                                reads=[rt], writes=[AT[jn]])
                if hf == 1:
                    SQF2 = [S.alloc(SCR + 32 * KIB + i * 8 * KIB, 8 * KIB) for i in range(2)]
                    SSQ2 = S.alloc(SCR + 48 * KIB, 8 * KIB)
                    SSB2 = S.alloc(SCR + 56 * KIB, 4 * KIB, BF16)
                for m in range(NCH):
                    w2_t, w2 = wload(w_ff2, 0, 4 * D, m * 128, 128)
                    for tl in range(2):
                        tb = hf * 2 + tl
                        sl = slice(tb * TB, (tb + 1) * TB)
                        ps = next_ps()
                        terms = [(w2[:, jn, :], AT[jn].ap[:, tl * TB:(tl + 1) * TB]) for jn in range(32)]
                        mm_group(ps, ps.ap[:, :], terms, reads=AT + w2_t)
                        S.op("dve", lambda e, ps=ps, m=m, sl=sl: e.tensor_tensor(
                            out=XT[m].ap[:, sl], in0=ps.ap[:, :], in1=XT[m].ap[:, sl], op=ALU.add),
                            reads=[ps, XT[m]], writes=[XT[m]])
                    if hf == 1 and m >= 1:
                        stats_acc(XT[m - 1], SQF2[(m - 1) % 2], SSQ2, m - 1 == 0)
                if hf == 1:
                    stats_acc(XT[NCH - 1], SQF2[(NCH - 1) % 2], SSQ2, False)

            checkpoint(6, XT)
            o = SCR
            RSTD2 = S.alloc(o, 8 * KIB); o += 8 * KIB
            OUT = [S.alloc(o + i * 8 * KIB, 8 * KIB) for i in range(2)]; o += 16 * KIB
            stats_acc_finish(SSQ2, SSB2, RSTD2)
            for c in range(NCH):
                ot = OUT[c % 2]
                S.op("dve", lambda e, c=c, ot=ot: e.scalar_tensor_tensor(out=ot.ap[:, :], in0=XT[c].ap[:, :],
                                                                         scalar=par(P_GF + c), in1=RSTD2.ap[:, :],
                                                                         op0=ALU.mult, op1=ALU.mult),
                     reads=[XT[c], PAR, RSTD2], writes=[ot])
                sp_dma(osems[c % 2], [(outT[c * 128:(c + 1) * 128, :], ot.ap[:, :])], reads=[ot])
            S.final_waits = [(ds.sem, ds.count) for ds in osems]

        except _Stop:
            pass

        with nc.Block() as block:
            @block.tensor
            def _(e):
                S.emit("pe", e)

            @block.scalar
            def _(e):
                S.emit("act", e)

            @block.vector
            def _(e):
                S.emit("dve", e)

            @block.gpsimd
            def _(e):
                S.emit("pool", e)

            @block.sync
            def _(e):
                S.emit("sp", e)
    return nc


def _cols(v, n):
    return np.ascontiguousarray(np.asarray(v, np.float32).reshape(n, 128).T)


def _build_t2(rpb):
    rpb = np.asarray(rpb, np.float32)
    kc = np.arange(64)[:, None]
    qc = np.arange(64)[None, :]
    dcol = np.clip(kc - qc, -15, 15) + 15
    ws = np.clip(qc - 8, 0, 48)
    valid = (kc >= ws) & (kc < ws + 16)
    t2 = np.empty((4, 2, 64, 2, 14, 64), np.float32)
    for a in range(2):
        for e in range(14):
            g = rpb[:, e + a][:, dcol]
            g = np.where(valid[None], g, np.float32(NEG))
            t2[:, a, :, :, e, :] = g.reshape(4, 2, 64, 64).transpose(0, 2, 1, 3)
    return np.ascontiguousarray(t2.reshape(4, 128, 2 * 14 * 64))


_NC_CACHE = {}


def kernel(x, ln1_g, w_in, b_in, rpb, w_att_o, conv_w, conv_b, w_rg_a, b_rg_a, w_rg_i, b_rg_i,
           lru_lambda, w_rec_o, w_out, ln2_g, w_ff1, w_ff2, lnf_g):
    f = lambda a: np.ascontiguousarray(np.asarray(a, np.float32))
    x = f(x)
    B = x.shape[0]
    params = np.zeros((128, NPAR), np.float32)
    params[:, P_BIN:P_BIN + 44] = _cols(f(b_in)[0], 44)
    params[:, P_G1:P_G1 + 8] = _cols(f(ln1_g)[0], 8)
    params[:, P_G2:P_G2 + 8] = _cols(f(ln2_g)[0], 8)
    params[:, P_GF:P_GF + 8] = _cols(f(lnf_g), 8)
    cw = f(conv_w)[0]
    for j in range(4):
        params[:, P_CW + j * 8:P_CW + (j + 1) * 8] = _cols(cw[j], 8)
    params[:, P_CB:P_CB + 8] = _cols(f(conv_b)[0], 8)
    for d in range(2):
        params[:, P_BA + d * 8:P_BA + (d + 1) * 8] = _cols(f(b_rg_a)[0, d], 8)
        params[:, P_BI + d * 8:P_BI + (d + 1) * 8] = _cols(f(b_rg_i)[0, d], 8)
        params[:, P_LAM + d * 8:P_LAM + (d + 1) * 8] = _cols(f(lru_lambda)[0, d], 8)
    wa = f(w_rg_a)[0]
    wi = f(w_rg_i)[0]
    w_rg = np.ascontiguousarray(np.stack([wa[0], wi[0], wa[1], wi[1]], axis=0))
    t2 = _build_t2(f(rpb)[0])
    shared = {
        "w_in": f(w_in)[0], "w_att_o": f(w_att_o)[0], "w_rec_o": f(w_rec_o)[0], "w_out": f(w_out)[0],
        "w_ff1": f(w_ff1)[0], "w_ff2": f(w_ff2)[0], "w_rg": w_rg, "params": params, "t2": t2,
        "ident": np.eye(128, dtype=np.float32),
    }
    in_maps = []
    for b in range(B):
        m = dict(shared)
        m["xT"] = np.ascontiguousarray(x[b].T)
        in_maps.append(m)
    if "nc" not in _NC_CACHE:
        _NC_CACHE["nc"] = build_program()
    nc = _NC_CACHE["nc"]
    res = run_bass_kernel_spmd(nc, in_maps, core_ids=list(range(B)))
    out = np.stack([np.ascontiguousarray(r["outT"].T) for r in res.results], axis=0)
    return out.astype(np.float32)
```

```python
import numpy as np
from contextlib import ExitStack

import concourse.bass as bass
import concourse.mybir as mybir
from concourse.bass_utils import run_bass_kernel_spmd
from concourse.ap import AP

F32 = mybir.dt.float32
BF16 = mybir.dt.bfloat16
AF = mybir.ActivationFunctionType
ALU = mybir.AluOpType

T = 2048
D = 1024
NCH = 8
TB = 512
NTB = T // TB
D_IN = 5632
EPS = 1e-6
GRID_W = 64
ROWS = T // GRID_W
NEG = -30000.0

P_BIN = 0
P_G1 = 44
P_G2 = 52
P_GF = 60
P_CW = 68
P_CB = 100
P_BA = 108
P_BI = 124
P_LAM = 140
NPAR = 156

Z_Q, Z_K, Z_V, Z_U, Z_Y, Z_GA, Z_GR = 0, 512, 1024, 1536, 2560, 3584, 4608

KIB = 1024
ARENA_BYTES = 207 * KIB


class Tile:
    __slots__ = ("ap", "lo", "hi", "wr", "rd")

    def __init__(self, ap, lo, hi):
        self.ap = ap
        self.lo = lo
        self.hi = hi
        self.wr = {}
        self.rd = {}


class DmaSem:
    def __init__(self, sem):
        self.sem = sem
        self.count = 0


class Sched:
    ENG = ("pe", "act", "dve", "pool", "sp")

    def __init__(self, nc, es):
        self.nc = nc
        self.es = es
        self.sem = {n: es.enter_context(nc.semaphore("s_" + n)) for n in self.ENG}
        self.count = {n: 0 for n in self.ENG}
        self.ops = {n: [] for n in self.ENG}
        self.tiles = []
        self.arena = None
        self.n_dsem = 0
        self.final_waits = []

    def set_arena(self, arena):
        self.arena = arena

    def alloc(self, lo, nbytes, dtype=F32, parts=128):
        assert lo % 4 == 0 and nbytes % 4 == 0
        hi = lo + nbytes
        assert hi <= ARENA_BYTES, (lo, nbytes)
        ap = self.arena[0:parts, lo // 4:hi // 4]
        if dtype == BF16:
            ap = ap.bitcast(BF16)
        t = Tile(ap, lo, hi)
        for o in self.tiles:
            if o.lo < hi and lo < o.hi:
                for src in (o.wr, o.rd):
                    for s, v in src.items():
                        if t.wr.get(s, (None, 0))[1] < v[1]:
                            t.wr[s] = v
        self.tiles.append(t)
        return t

    def extern_tile(self, ap):
        return Tile(ap, -1, -1)

    def dma_sem(self, name=None):
        self.n_dsem += 1
        s = self.es.enter_context(self.nc.semaphore(name or ("d%d" % self.n_dsem)))
        return DmaSem(s)

    def _deps(self, eng, reads, writes):
        own = id(self.sem[eng])
        deps = {}

        def add(k, v):
            if deps.get(k, (None, 0))[1] < v[1]:
                deps[k] = v

        for t in reads:
            for k, v in t.wr.items():
                add(k, v)
        for t in writes:
            for k, v in t.wr.items():
                add(k, v)
            for k, v in t.rd.items():
                add(k, v)
        if eng == "pe":
            deps.pop(own, None)
        return list(deps.values())

    def op(self, eng, fn, reads=(), writes=()):
        deps = self._deps(eng, reads, writes)
        self.count[eng] += 1
        c = self.count[eng]
        sem = self.sem[eng]
        self.ops[eng].append((fn, deps, (sem, 1)))
        k = id(sem)
        for t in reads:
            t.rd[k] = (sem, c)
        for t in writes:
            t.wr[k] = (sem, c)

    def dma(self, queue, dsem, pairs, reads=(), writes=(), extra_deps=()):
        deps = self._deps(queue, reads, writes) + list(extra_deps)
        n = len(pairs)
        dsem.count += 16 * n
        c = dsem.count
        sem = dsem.sem

        def fn(e, pairs=pairs, sem=sem):
            for o, i in pairs:
                e.dma_start(out=o, in_=i).then_inc(sem, 16)
            return None

        self.ops[queue].append((fn, deps, None))
        k = id(sem)
        for t in reads:
            t.rd[k] = (sem, c)
        for t in writes:
            t.wr[k] = (sem, c)

    def emit(self, eng, e):
        seen = {}
        for fn, deps, sig in self.ops[eng]:
            for sem, v in deps:
                if seen.get(id(sem), 0) < v:
                    e.wait_ge(sem, v)
                    seen[id(sem)] = v
            ins = fn(e)
            if sig is not None:
                ins.then_inc(sig[0], sig[1])
        if eng == "sp":
            for sem, v in self.final_waits:
                e.wait_ge(sem, v)


class _Stop(Exception):
    pass


def build_program(stop=None):
    nc = bass.Bass("TRN2", target_bir_lowering=False)
    es = ExitStack()
    with es:
        es.enter_context(nc.allow_low_precision("bf16 matmul operands, fp32 accumulation"))
        xT = nc.dram_tensor("xT", [D, T], F32, kind="ExternalInput").ap()
        w_in = nc.dram_tensor("w_in", [D, D_IN], F32, kind="ExternalInput").ap()
        w_att_o = nc.dram_tensor("w_att_o", [512, D], F32, kind="ExternalInput").ap()
        w_rec_o = nc.dram_tensor("w_rec_o", [D, D], F32, kind="ExternalInput").ap()
        w_out = nc.dram_tensor("w_out", [D, D], F32, kind="ExternalInput").ap()
        w_ff1 = nc.dram_tensor("w_ff1", [D, 4 * D], F32, kind="ExternalInput").ap()
        w_ff2 = nc.dram_tensor("w_ff2", [4 * D, D], F32, kind="ExternalInput").ap()
        w_rg = nc.dram_tensor("w_rg", [4, 16, 64, 64], F32, kind="ExternalInput").ap()
        params = nc.dram_tensor("params", [128, NPAR], F32, kind="ExternalInput").ap()
        t2d = nc.dram_tensor("t2", [4, 128, 2 * 14 * 64], F32, kind="ExternalInput").ap()
        identd = nc.dram_tensor("ident", [128, 128], F32, kind="ExternalInput").ap()
        outT = nc.dram_tensor("outT", [D, T], F32, kind="ExternalOutput").ap()

        arena = es.enter_context(nc.sbuf_tensor("arena", [128, ARENA_BYTES // 4], F32))
        S = Sched(nc, es)
        S.set_arena(arena)
        psum = []
        psall = es.enter_context(nc.psum_tensor("psall", [128, 8 * TB], F32))
        for i in range(8):
            psum.append(Tile(psall[:, i * TB:(i + 1) * TB], -1, -1))

        def pair_ap(i):
            return psall[:, 2 * i * TB:(2 * i + 2) * TB]
        ps_rr = [0]

        ps_excl = set()

        def next_ps():
            while True:
                i = ps_rr[0] % 8
                ps_rr[0] += 1
                if i not in ps_excl:
                    return psum[i]

        def next_pair():
            while True:
                i = ps_rr[0] % 8
                if i % 2:
                    ps_rr[0] += 1
                    continue
                ps_rr[0] += 2
                if i not in ps_excl and (i + 1) not in ps_excl:
                    return psum[i], psum[i + 1], pair_ap(i // 2)

        off = 0
        PAR = S.alloc(off, NPAR * 4); off += NPAR * 4
        ONES = S.alloc(off, 128 * 2, BF16); off += 256
        CL = S.alloc(off, 16 * 4); off += 64
        CL2 = S.alloc(off, 16 * 4); off += 64
        LTMP = S.alloc(off, 16 * 4); off += 64
        off = 2 * KIB
        NSUB = 16
        SUB = 2 * KIB
        wsub = [S.alloc(off + i * SUB, SUB, BF16) for i in range(NSUB)]
        wsems = [S.dma_sem("wsem%d" % i) for i in range(NSUB)]
        w_ptr = [0]
        w_hist = []
        W_DEPTH = 4

        sp_hist = []
        SP_DEPTH = 4

        def sp_dma(dsem, pairs, reads=(), writes=()):
            extra = [sp_hist[-SP_DEPTH]] if len(sp_hist) >= SP_DEPTH else []
            S.dma("sp", dsem, pairs, reads=reads, writes=writes, extra_deps=extra)
            sp_hist.append((dsem.sem, dsem.count))

        pool_gate = []

        def pool_dma(dsem, pairs, writes):
            extra = [w_hist[-W_DEPTH]] if len(w_hist) >= W_DEPTH else []
            extra += pool_gate
            del pool_gate[:]
            S.dma("pool", dsem, pairs, writes=writes, extra_deps=extra)
            w_hist.append((dsem.sem, dsem.count))
        off += NSUB * SUB
        HT_OFF = off
        off += 32 * KIB
        REC_OFF = off
        off += 32 * KIB
        ATT_OFF = off
        off += 16 * KIB
        SCR = off
        assert SCR == 114 * KIB

        def par(col, n=1):
            return PAR.ap[:, col:col + n]

        def wload(w2d, r0, nr, c0, ncols):
            nk = nr // 128
            nbytes = nk * ncols * 2
            k = 1 if nbytes <= SUB else (2 if nbytes <= 2 * SUB else 4)
            assert nbytes <= k * SUB
            p = w_ptr[0]
            if p % k:
                p += k - p % k
            i0 = p % NSUB
            w_ptr[0] = p + k
            tiles = wsub[i0:i0 + k]
            lo = wsub[i0].lo
            ap = S.arena[:, lo // 4:(lo + k * SUB) // 4].bitcast(BF16)
            view = ap[:, 0:nk * ncols].rearrange("p (k c) -> p k c", k=nk)
            src = w2d[r0:r0 + nr, c0:c0 + ncols].rearrange("(k p) c -> p k c", p=128)
            pool_dma(wsems[i0], [(view, src)], tiles)
            return tiles, view

        def mm_group(out_tile, out_ap, terms, reads):
            def fn(e, terms=terms, out_ap=out_ap):
                n = len(terms)
                ins = None
                for i, (l, r) in enumerate(terms):
                    ins = e.matmul(out_ap, l, r, start=(i == 0), stop=(i == n - 1))
                return ins
            S.op("pe", fn, reads=reads, writes=[out_tile])

        csem = S.dma_sem("csem")
        S.dma("sp", csem, [(PAR.ap[:, :], params[:, :])], writes=[PAR])
        S.op("dve", lambda e: e.memset(ONES.ap[:, :], 1.0), writes=[ONES])
        EPSC = S.alloc(1600, 4)
        ONEC = S.alloc(1604, 4)
        S.op("dve", lambda e: e.memset(EPSC.ap[:, :], EPS), writes=[EPSC])
        S.op("dve", lambda e: e.memset(ONEC.ap[:, :], 1.0), writes=[ONEC])
        S.op("act", lambda e: e.activation(out=LTMP.ap[:, :], in_=par(P_LAM, 16), func=AF.Exp, scale=-1.0),
             reads=[PAR], writes=[LTMP])
        S.op("act", lambda e: e.activation(out=LTMP.ap[:, :], in_=LTMP.ap[:, :], func=AF.Ln, bias=ONEC.ap[:, 0:1]),
             reads=[LTMP, ONEC], writes=[LTMP])
        S.op("dve", lambda e: e.tensor_scalar(out=CL.ap[:, :], in0=LTMP.ap[:, :], scalar1=-8.0, scalar2=None,
                                              op0=ALU.mult), reads=[LTMP], writes=[CL])
        S.op("dve", lambda e: e.tensor_scalar(out=CL2.ap[:, :], in0=LTMP.ap[:, :], scalar1=-16.0, scalar2=None,
                                              op0=ALU.mult), reads=[LTMP], writes=[CL2])
        IDENT = S.alloc(1100, 256, BF16)
        HB = S.alloc(1360, 128)
        CLH = S.alloc(1488, 64)
        QTR = S.alloc(1552, 4)
        isem = S.dma_sem("isem")
        pool_dma(isem, [(IDENT.ap[:, :], identd[:, :])], [IDENT])
        S.op("dve", lambda e: e.memset(QTR.ap[:, :], 0.25), writes=[QTR])
        S.op("dve", lambda e: e.tensor_scalar(out=HB.ap[:, :], in0=par(P_BA, 32), scalar1=0.5, scalar2=None,
                                              op0=ALU.mult), reads=[PAR], writes=[HB])
        S.op("dve", lambda e: e.tensor_scalar(out=CLH.ap[:, :], in0=LTMP.ap[:, :], scalar1=-4.0, scalar2=None,
                                              op0=ALU.mult), reads=[LTMP], writes=[CLH])

        xl_sems = [S.dma_sem("xl%d" % c) for c in range(NCH)]

        def stats_chunk(xt, sq, banks, first, last, eng="act"):
            if eng == "act":
                S.op("act", lambda e: e.activation(out=sq.ap[:, :], in_=xt.ap[:, :], func=AF.Square),
                     reads=[xt], writes=[sq])
            else:
                S.op("dve", lambda e: e.tensor_tensor(out=sq.ap[:, :], in0=xt.ap[:, :], in1=xt.ap[:, :], op=ALU.mult),
                     reads=[xt], writes=[sq])
            for tb in range(NTB):
                def fn(e, tb=tb):
                    return e.matmul(banks[tb].ap[:, :], ONES.ap[:, :], sq.ap[:, tb * TB:(tb + 1) * TB],
                                    start=first, stop=last)
                S.op("pe", fn, reads=[ONES, sq], writes=[banks[tb]])

        def stats_finish(RSTD, banks):
            for tb in range(NTB):
                sl = slice(tb * TB, (tb + 1) * TB)
                S.op("act", lambda e, tb=tb, sl=sl: e.activation(out=RSTD.ap[:, sl], in_=banks[tb].ap[:, :],
                                                                  func=AF.Ln, scale=1.0 / D, bias=EPSC.ap[:, 0:1]),
                     reads=[banks[tb], EPSC], writes=[RSTD])
            S.op("act", lambda e: e.activation(out=RSTD.ap[:, :], in_=RSTD.ap[:, :], func=AF.Exp, scale=-0.5),
                 reads=[RSTD], writes=[RSTD])

        def stats_acc(xt, sqf, ssq, first):
            S.op("act", lambda e: e.activation(out=sqf.ap[:, :], in_=xt.ap[:, :], func=AF.Square),
                 reads=[xt], writes=[sqf])
            if first:
                S.op("dve", lambda e: e.tensor_copy(out=ssq.ap[:, :], in_=sqf.ap[:, :]), reads=[sqf], writes=[ssq])
            else:
                S.op("dve", lambda e: e.tensor_tensor(out=ssq.ap[:, :], in0=ssq.ap[:, :], in1=sqf.ap[:, :], op=ALU.add),
                     reads=[ssq, sqf], writes=[ssq])

        def stats_acc_finish(ssq, ssb, RSTD):
            S.op("dve", lambda e: e.tensor_copy(out=ssb.ap[:, :], in_=ssq.ap[:, :]), reads=[ssq], writes=[ssb])
            banks = [next_ps() for _ in range(NTB)]
            for tb in range(NTB):
                def fn(e, tb=tb):
                    return e.matmul(banks[tb].ap[:, :], ONES.ap[:, :], ssb.ap[:, tb * TB:(tb + 1) * TB],
                                    start=True, stop=True)
                S.op("pe", fn, reads=[ONES, ssb], writes=[banks[tb]])
            stats_finish(RSTD, banks)

        def rms_stats(XT, sq_tiles, RSTD):
            banks = [next_ps() for _ in range(NTB)]
            for c in range(NCH):
                stats_chunk(XT[c], sq_tiles[c % len(sq_tiles)], banks, c == 0, c == NCH - 1,
                            eng=("act" if c % 2 == 0 else "dve"))
            stats_finish(RSTD, banks)


        dsem_dbg = S.dma_sem("dbg")

        def checkpoint(k, tiles):
            if stop != k:
                return
            for i, t in enumerate(tiles):
                S.dma("pool", dsem_dbg, [(outT[i * 128:(i + 1) * 128, :], t.ap[:, 0:T])], reads=[t])
            S.final_waits = [(dsem_dbg.sem, dsem_dbg.count)]
            raise _Stop()

        try:
            HT = [S.alloc(HT_OFF + c * 4 * KIB, 4 * KIB, BF16) for c in range(NCH)]
            XT0 = [S.alloc(SCR + c * 8 * KIB, 8 * KIB) for c in range(NCH)]
            SQ0 = [S.alloc(SCR + 64 * KIB + i * 4 * KIB, 4 * KIB, BF16) for i in range(2)]
            RSTD0 = S.alloc(SCR + 72 * KIB, 8 * KIB)
            for c in range(NCH):
                sp_dma(xl_sems[c], [(XT0[c].ap[:, :], xT[c * 128:(c + 1) * 128, :])], writes=[XT0[c]])
            rms_stats(XT0, SQ0, RSTD0)
            HTS = [[Tile(HT[c].ap[:, tb * TB:(tb + 1) * TB], -1, -1) for tb in range(NTB)] for c in range(NCH)]
            for tb in range(NTB):
                for c in range(NCH):
                    sl = slice(tb * TB, (tb + 1) * TB)
                    S.op("dve", lambda e, c=c, sl=sl: e.scalar_tensor_tensor(
                        out=HT[c].ap[:, sl], in0=XT0[c].ap[:, sl], scalar=par(P_G1 + c), in1=RSTD0.ap[:, sl],
                        op0=ALU.mult, op1=ALU.mult), reads=[XT0[c], PAR, RSTD0], writes=[HT[c], HTS[c][tb]])

            def proj_fm(wview, ncol_lo, k_tiles, nk, tb):
                sl = slice(tb * TB, (tb + 1) * TB)
                return [(wview[:, k, ncol_lo:ncol_lo + 128], k_tiles[k].ap[:, sl]) for k in range(nk)]

            checkpoint(0, HT)
            RECT = [S.alloc(REC_OFF + c * 4 * KIB, 4 * KIB, BF16) for c in range(NCH)]
            WG = S.alloc(ATT_OFF, 8 * KIB, BF16)
            wgv = WG.ap[:, :].rearrange("p (g c m) -> p g c m", g=4, c=8)
            S.op("pool", lambda e: e.memset(WG.ap[:, :], 0.0), writes=[WG])
            wgsem = S.dma_sem("wgsem")
            for g in range(4):
                pairs = []
                for a in range(2):
                    src = w_rg[g, a:16:2, :, :].rearrange("c p d -> p c d")
                    dst = wgv[a * 64:(a + 1) * 64, g, :, a * 64:(a + 1) * 64]
                    pairs.append((dst, src))
                pool_dma(wgsem, pairs, [WG])
            DGC = [S.alloc(ATT_OFF + 12 * KIB + i * KIB, KIB, BF16) for i in range(2)]
            o = SCR
            UW = 2052
            UB_ = [S.alloc(o + i * UW * 2, UW * 2, BF16) for i in range(2)]; o += 2 * UW * 2
            GY_ = [S.alloc(o + i * 4 * KIB, 4 * KIB, BF16) for i in range(3)]; o += 12 * KIB
            UC_ = [S.alloc(o + i * 8 * KIB, 8 * KIB) for i in range(2)]; o += 16 * KIB
            UCB_ = [S.alloc(ATT_OFF + 8 * KIB, 4 * KIB, BF16), S.alloc(o, 4 * KIB, BF16)]; o += 4 * KIB
            NSET = 3
            HALF = T // 2
            SA = [S.alloc(o + i * 4 * KIB, 4 * KIB) for i in range(NSET)]; o += NSET * 4 * KIB
            SM = [S.alloc(o + i * 4 * KIB, 4 * KIB) for i in range(NSET)]; o += NSET * 4 * KIB
            ST = [S.alloc(o + i * 4 * KIB, 4 * KIB) for i in range(NSET)]; o += NSET * 4 * KIB
            HH = [[S.alloc(o + (d * 2 + h) * 4 * KIB, 4 * KIB) for h in range(2)] for d in range(2)]; o += 16 * KIB
            assert o <= ARENA_BYTES, o

            def rev(ap2d, n):
                pst = ap2d.ap[0][0]
                npart = ap2d.ap[0][1]
                return AP(ap2d.tensor, ap2d.offset + (n - 1), [[pst, npart], [-1, n]])

            lw = {}
            for c in range(NCH):
                if c == 1:
                    pool_gate.extend((xs.sem, xs.count) for xs in xl_sems)
                lw[("u", c)] = wload(w_in, 0, D, Z_U + c * 128, 128)
                lw[("y", c)] = wload(w_in, 0, D, Z_Y + c * 128, 128)
            for i in range(2):
                S.op("pool", lambda e, i=i: e.memset(UB_[i].ap[:, :], 0.0), writes=[UB_[i]])

            def lru_s1(c):
                wu_t, wu = lw[("u", c)]
                wy_t, wy = lw[("y", c)]
                UB, GY = UB_[c % 2], GY_[c % 3]
                dg = DGC[c % 2]
                dgv = dg.ap[:, :].rearrange("p (j m) -> p j m", j=4)
                for j in range(4):
                    S.op("dve", lambda e, c=c, j=j, dgv=dgv: e.tensor_scalar(
                        out=dgv[:, j, :], in0=IDENT.ap[:, :], scalar1=par(P_CW + j * 8 + c), scalar2=None,
                        op0=ALU.mult), reads=[IDENT, PAR], writes=[dg])
                for hb in range(2):
                    pa, pb, pap = next_pair()
                    for tl, ps in ((0, pa), (1, pb)):
                        mm_group(ps, ps.ap[:, :], proj_fm(wu, 0, HT, NCH, 2 * hb + tl), reads=(([HTS[k][2 * hb + tl] for k in range(NCH)] if c == 0 else HT) + wu_t))
                    S.op("dve", lambda e, pap=pap, hb=hb, UB=UB, c=c: e.tensor_scalar(
                        out=UB.ap[:, 2 + hb * 2 * TB:2 + (hb + 1) * 2 * TB], in0=pap,
                        scalar1=par(P_BIN + Z_U // 128 + c), scalar2=None, op0=ALU.add),
                        reads=[pa, pb, PAR], writes=[UB])
                for hb in range(2):
                    pa, pb, pap = next_pair()
                    for tl, ps in ((0, pa), (1, pb)):
                        mm_group(ps, ps.ap[:, :], proj_fm(wy, 0, HT, NCH, 2 * hb + tl), reads=(([HTS[k][2 * hb + tl] for k in range(NCH)] if c == 0 else HT) + wy_t))
                    S.op("act", lambda e, pap=pap, hb=hb, GY=GY, c=c: e.activation(
                        out=GY.ap[:, hb * 2 * TB:(hb + 1) * 2 * TB], in_=pap, func=AF.Gelu_apprx_tanh,
                        bias=par(P_BIN + Z_Y // 128 + c)),
                        reads=[pa, pb, PAR], writes=[GY])

            def lru_s2(c):
                UB, UC, dg, UCB = UB_[c % 2], UC_[c % 2], DGC[c % 2], UCB_[c % 2]
                dgv = dg.ap[:, :].rearrange("p (j m) -> p j m", j=4)
                for hb in range(2):
                    pa, pb, pap = next_pair()
                    for tl, ps in ((0, pa), (1, pb)):
                        tb = 2 * hb + tl
                        terms = [(dgv[:, j, :], UB.ap[:, tb * TB + j:tb * TB + j + TB]) for j in range(4)]
                        mm_group(ps, ps.ap[:, :], terms, reads=[dg, UB])
                    S.op("dve", lambda e, pap=pap, hb=hb, UC=UC, c=c: e.tensor_scalar(
                        out=UC.ap[:, hb * 2 * TB:(hb + 1) * 2 * TB], in0=pap, scalar1=par(P_CB + c), scalar2=None,
                        op0=ALU.add), reads=[pa, pb, PAR], writes=[UC])
                S.op("dve", lambda e, UC=UC: e.tensor_copy(out=UCB.ap[:, :], in_=UC.ap[:, :]),
                     reads=[UC], writes=[UCB])

            ITEMS = [(0, 0), (1, 1), (0, 1), (1, 0)]
            item_ctr = [0]

            cur = {}

            def lru_s3(c, p):
                UCB = UCB_[c % 2]
                its = []
                for (d, h) in ITEMS[2 * p:2 * p + 2]:
                    s_ = item_ctr[0] % NSET
                    item_ctr[0] += 1
                    its.append((d, h, SA[s_], SM[s_], ST[s_]))
                cur[(c, p)] = its
                for (d, h, A, M, TI) in its:
                    for gi, G in ((0, A), (1, TI)):
                        g = d * 2 + gi
                        hbcol = gi * 16 + d * 8 + c
                        pa, pb, pap = next_pair()
                        for tl, ps in ((0, pa), (1, pb)):
                            tb = 2 * h + tl
                            mm_group(ps, ps.ap[:, :], [(wgv[:, g, c, :], UCB.ap[:, tb * TB:(tb + 1) * TB])], reads=[WG, UCB])
                        S.op("act", lambda e, pap=pap, G=G, hbcol=hbcol: e.activation(
                            out=G.ap[:, :], in_=pap, func=AF.Tanh, scale=0.5,
                            bias=HB.ap[:, hbcol:hbcol + 1]), reads=[pa, pb, HB], writes=[G])

            def lru_s4(c, p):
                its = cur[(c, p)]
                for (d, h, A, M, TI) in its:
                    ccol = d * 8 + c
                    S.op("act", lambda e, A=A, M=M, ccol=ccol: e.activation(
                        out=M.ap[:, :], in_=A.ap[:, :], func=AF.Exp,
                        scale=CL.ap[:, ccol:ccol + 1], bias=CL.ap[:, ccol:ccol + 1]),
                        reads=[A, CL], writes=[M])
                    S.op("act", lambda e, A=A, ccol=ccol: e.activation(
                        out=A.ap[:, :], in_=A.ap[:, :], func=AF.Exp,
                        scale=CLH.ap[:, ccol:ccol + 1], bias=CLH.ap[:, ccol:ccol + 1]),
                        reads=[A, CLH], writes=[A])
                for (d, h, A, M, TI) in its:
                    S.op("act", lambda e, M=M: e.activation(out=M.ap[:, :], in_=M.ap[:, :], func=AF.Sqrt,
                                                            scale=-0.25, bias=QTR.ap[:, 0:1]),
                         reads=[M, QTR], writes=[M])

            def lru_s5(c, p):
                UC = UC_[c % 2]
                its = cur[(c, p)]
                for (d, h, A, M, TI) in its:
                    hs = slice(h * HALF, (h + 1) * HALF)
                    S.op("dve", lambda e, TI=TI, UC=UC, hs=hs: e.scalar_tensor_tensor(
                        out=TI.ap[:, :], in0=TI.ap[:, :], scalar=1.0, in1=UC.ap[:, hs], op0=ALU.add, op1=ALU.mult),
                        reads=[TI, UC], writes=[TI])
                    S.op("dve", lambda e, TI=TI, M=M: e.tensor_tensor(out=TI.ap[:, :], in0=TI.ap[:, :], in1=M.ap[:, :],
                                                                      op=ALU.mult), reads=[TI, M], writes=[TI])
                    Hout = HH[d][h]
                    if d == 0:
                        if h == 0:
                            S.op("dve", lambda e, A=A, TI=TI, Hout=Hout: e.tensor_tensor_scan(
                                out=Hout.ap[:, :], data0=A.ap[:, :], data1=TI.ap[:, :], initial=0.0,
                                op0=ALU.mult, op1=ALU.add), reads=[A, TI], writes=[Hout])
                        else:
                            prev = HH[0][0]
                            S.op("dve", lambda e, A=A, TI=TI, Hout=Hout, prev=prev: e.tensor_tensor_scan(
                                out=Hout.ap[:, :], data0=A.ap[:, :], data1=TI.ap[:, :],
                                initial=prev.ap[:, HALF - 1:HALF], op0=ALU.mult, op1=ALU.add),
                                reads=[A, TI, prev], writes=[Hout])
                    else:
                        if h == 1:
                            S.op("dve", lambda e, A=A, TI=TI, Hout=Hout: e.tensor_tensor_scan(
                                out=rev(Hout.ap[:, :], HALF), data0=rev(A.ap[:, :], HALF), data1=rev(TI.ap[:, :], HALF),
                                initial=0.0, op0=ALU.mult, op1=ALU.add), reads=[A, TI], writes=[Hout])
                        else:
                            prev = HH[1][1]
                            S.op("dve", lambda e, A=A, TI=TI, Hout=Hout, prev=prev: e.tensor_tensor_scan(
                                out=rev(Hout.ap[:, :], HALF), data0=rev(A.ap[:, :], HALF), data1=rev(TI.ap[:, :], HALF),
                                initial=prev.ap[:, 0:1], op0=ALU.mult, op1=ALU.add),
                                reads=[A, TI, prev], writes=[Hout])

            def lru_s6(c, h):
                GY = GY_[c % 3]
                hs = slice(h * HALF, (h + 1) * HALF)
                S.op("dve", lambda e, h=h: e.tensor_tensor(out=HH[0][h].ap[:, :], in0=HH[0][h].ap[:, :],
                                                         in1=HH[1][h].ap[:, :], op=ALU.add),
                     reads=[HH[0][h], HH[1][h]], writes=[HH[0][h]])
                S.op("dve", lambda e, c=c, h=h, GY=GY, hs=hs: e.tensor_tensor(
                    out=RECT[c].ap[:, hs], in0=HH[0][h].ap[:, :], in1=GY.ap[:, hs], op=ALU.mult),
                    reads=[HH[0][h], GY], writes=[RECT[c]])

            lru_s1(0)
            lru_s2(0)
            for c in range(NCH):
                lru_s3(c, 0)
                lru_s4(c, 0)
                if c + 1 < NCH:
                    lru_s1(c + 1)
                lru_s5(c, 0)
                lru_s3(c, 1)
                if c + 1 < NCH:
                    lru_s2(c + 1)
                lru_s4(c, 1)
                lru_s5(c, 1)
                lru_s6(c, 1)
                lru_s6(c, 0)

            checkpoint(1, RECT)
            ATT = [S.alloc(ATT_OFF + j * 4 * KIB, 4 * KIB, BF16) for j in range(4)]
            o = SCR
            QT = [S.alloc(o + j * 4 * KIB, 4 * KIB, BF16) for j in range(4)]; o += 16 * KIB
            KT = [S.alloc(o + j * 4 * KIB, 4 * KIB, BF16) for j in range(4)]; o += 16 * KIB
            V1 = S.alloc(o, 16 * KIB, BF16); o += 16 * KIB
            V2 = S.alloc(o, 15 * KIB, BF16); o += 15 * KIB
            T2 = [S.alloc(o + i * 3584, 3584, BF16) for i in range(2)]; o += 7 * KIB
            PT = [S.alloc(o + i * 2 * KIB, 2 * KIB, BF16) for i in range(2)]; o += 4 * KIB
            NRM = [S.alloc(o + i * KIB, KIB) for i in range(2)]; o += 2 * KIB
            RDN = [S.alloc(o + i * KIB, KIB) for i in range(2)]; o += 2 * KIB
            assert o <= ARENA_BYTES
            t2sems = [S.dma_sem("t2s%d" % i) for i in range(2)]
            v1v = V1.ap[:, :].rearrange("p (b f) -> p b f", b=16)
            v2v = V2.ap[:, :].rearrange("p (b f) -> p b f", b=15)

            wv_t, wv = wload(w_in, 0, D, Z_V, 512)
            for (vv, Vt, nb, toff) in ((v1v, V1, 16, 0),):
                for b in range(nb):
                    t0 = toff + b * 128
                    ps = next_ps()
                    terms = [(HT[k].ap[:, t0:t0 + 128], wv[:, k, :]) for k in range(NCH)]
                    mm_group(ps, ps.ap[:, :], terms, reads=HT + wv_t)
                    if b % 2 == 0:
                        S.op("act", lambda e, ps=ps, vv=vv, b=b: e.activation(out=vv[:, b, :], in_=ps.ap[:, :], func=AF.Copy),
                             reads=[ps], writes=[Vt])
                    else:
                        S.op("dve", lambda e, ps=ps, vv=vv, b=b: e.tensor_copy(out=vv[:, b, :], in_=ps.ap[:, :]),
                             reads=[ps], writes=[Vt])

            v2sem = S.dma_sem("v2sem")
            sp_dma(v2sem, [(v2v[0:64, 0:15, :], v1v[64:128, 0:15, :]),
                           (v2v[64:128, 0:15, :], v1v[0:64, 1:16, :])], reads=[V1], writes=[V2])

            for j in range(4):
                wq_t, wq = wload(w_in, 0, D, Z_Q + j * 128, 128)
                wk_t, wk = wload(w_in, 0, D, Z_K + j * 128, 128)
                for tb in range(NTB):
                    sl = slice(tb * TB, (tb + 1) * TB)
                    ps = next_ps()
                    mm_group(ps, ps.ap[:, :], proj_fm(wq, 0, HT, NCH, tb), reads=HT + wq_t)
                    S.op("dve", lambda e, ps=ps, sl=sl, j=j: e.tensor_scalar(
                        out=QT[j].ap[:, sl], in0=ps.ap[:, :], scalar1=par(P_BIN + Z_Q // 128 + j), scalar2=0.125,
                        op0=ALU.add, op1=ALU.mult), reads=[ps, PAR], writes=[QT[j]])
                    ps = next_ps()
                    mm_group(ps, ps.ap[:, :], proj_fm(wk, 0, HT, NCH, tb), reads=HT + wk_t)
                    S.op("act", lambda e, ps=ps, sl=sl, j=j: e.activation(
                        out=KT[j].ap[:, sl], in_=ps.ap[:, :], func=AF.Identity, bias=par(P_BIN + Z_K // 128 + j)),
                        reads=[ps, PAR], writes=[KT[j]])

            units = [(j, qg, qp) for j in range(4) for qg in range(ROWS // 4) for qp in range(2)]
            ust = {}
            t2_state = {}

            def rec_S(ui):
                j, qg, qp = units[ui]
                if j not in t2_state:
                    t2t = T2[j % 2]
                    pool_dma(t2sems[j % 2], [(t2t.ap[:, :], t2d[j, :, :])], [t2t])
                    t2_state[j] = t2t
                qrs = [qg * 4 + qp * 2 + r for r in range(2)]
                rss = [min(max(qr - 4, 0), ROWS - 8) for qr in qrs]
                e0s = [rs - qr + 7 for rs, qr in zip(rss, qrs)]
                sbank = [psum[(2 * ui) % 4], psum[(2 * ui + 1) % 4]]
                ust[ui] = dict(qrs=qrs, rss=rss, e0s=e0s, sbank=sbank, pt=PT[ui % 2])
                t2t = t2_state[j]
                t2v = t2t.ap[:, :].rearrange("p (h e q) -> p h e q", h=2, e=14)

                def fn_s(e, j=j, qrs=qrs, rss=rss, sbank=sbank, e0s=e0s, t2v=t2v):
                    ins = None
                    for r in range(2):
                        for hh in range(2):
                            ins = e.matmul(sbank[hh].ap[:, r * 256:(r + 1) * 256], IDENT.ap[:, :],
                                           t2v[:, hh, e0s[r]:e0s[r] + 7:2, :], start=True, stop=False)
                        for m in range(4):
                            for hh in range(2):
                                pl = slice(hh * 64, (hh + 1) * 64)
                                k0 = (rss[r] + 2 * m) * 64
                                col = (r * 4 + m) * 64
                                ins = e.matmul(sbank[hh].ap[:, col:col + 64], KT[j].ap[pl, k0:k0 + 128],
                                               QT[j].ap[pl, qrs[r] * 64:(qrs[r] + 1) * 64],
                                               start=False, stop=(m == 3))
                    return ins
                S.op("pe", fn_s, reads=[KT[j], QT[j], t2t, IDENT], writes=sbank)

            def rec_mid(ui):
                st = ust[ui]
                sbank, pt = st["sbank"], st["pt"]
                pap = pair_ap(ui % 2)
                S.op("act", lambda e, pap=pap, pt=pt: e.activation(out=pt.ap[:, :], in_=pap, func=AF.Exp),
                     reads=list(sbank), writes=[pt])

            nd_state = {}

            def rec_PV(ui):
                j, qg, qp = units[ui]
                st = ust[ui]
                rss, pt = st["rss"], st["pt"]
                if (j, qg) not in nd_state:
                    nd_state[(j, qg)] = psum[4 + (j * (ROWS // 4) + qg) % 3]
                nd = nd_state[(j, qg)]

                def fn_pv(e, j=j, qp=qp, rss=rss, pt=pt, nd=nd):
                    ins = None
                    ptv = pt.ap[:, :].rearrange("p (h r m q) -> p h r m q", h=2, r=2, m=4)
                    for r in range(2):
                        qi = qp * 2 + r
                        for m in range(4):
                            b = rss[r] + 2 * m
                            for hh in range(2):
                                pl = slice(hh * 64, (hh + 1) * 64)
                                h = 2 * j + hh
                                vsrc = v1v[:, b // 2, h * 64:(h + 1) * 64] if b % 2 == 0 else \
                                    v2v[:, (b - 1) // 2, h * 64:(h + 1) * 64]
                                ins = e.matmul(nd.ap[pl, qi * 64:(qi + 1) * 64], vsrc, ptv[:, hh, r, m, :],
                                               start=(m == 0), stop=(m == 3))
                    for m in range(4):
                        for hh in range(2):
                            pl = slice(hh * 64, (hh + 1) * 64)
                            ins = e.matmul(nd.ap[pl, 256 + qp * 128:256 + (qp + 1) * 128], ONES.ap[:, 0:64], ptv[:, hh, :, m, :],
                                           start=(m == 0), stop=(m == 3))
                    return ins
                S.op("pe", fn_pv, reads=[pt, V1, V2, ONES], writes=[nd])
                if qp == 1:
                    nrm = NRM[qg % 2]
                    rdn = RDN[qg % 2]
                    nnum = nd.ap[:, 0:256]
                    nden = nd.ap[:, 256:512]

                    def n_act1(nd=nd, nden=nden, rdn=rdn):
                        S.op("act", lambda e: e.activation(out=rdn.ap[:, :], in_=nden, func=AF.Ln),
                             reads=[nd], writes=[rdn])
                        S.op("act", lambda e: e.activation(out=rdn.ap[:, :], in_=rdn.ap[:, :], func=AF.Exp, scale=-1.0),
                             reads=[rdn], writes=[rdn])

                    def n_dve(nd=nd, nnum=nnum, nrm=nrm, rdn=rdn):
                        S.op("dve", lambda e: e.tensor_tensor(out=nrm.ap[:, :], in0=nnum, in1=rdn.ap[:, :], op=ALU.mult),
                             reads=[nd, rdn], writes=[nrm])

                    def n_act2(nrm=nrm, j=j, qg=qg):
                        S.op("act", lambda e: e.activation(
                            out=ATT[j].ap[:, qg * 256:(qg + 1) * 256], in_=nrm.ap[:, :], func=AF.Identity,
                            bias=par(P_BIN + Z_V // 128 + j)), reads=[nrm, PAR], writes=[ATT[j]])
                    pend_a1.append((ui + 2, n_act1))
                    pend_d.append((ui + 2, n_dve))
                    pend_a2.append((ui + 3, n_act2))

            pend_a1, pend_d, pend_a2 = [], [], []

            def flush(lst, ui):
                while lst and lst[0][0] <= ui:
                    lst.pop(0)[1]()

            rec_S(0)
            for ui in range(len(units)):
                if ui + 1 < len(units):
                    rec_S(ui + 1)
                flush(pend_a2, ui)
                flush(pend_a1, ui)
                rec_mid(ui)
                flush(pend_d, ui)
                rec_PV(ui)
            for lst in (pend_a1, pend_d, pend_a2):
                flush(lst, 10 ** 9)

            checkpoint(2, ATT + QT)
            o = SCR
            MIX = [S.alloc(o + m * 4 * KIB, 4 * KIB, BF16) for m in range(NCH)]; o += 32 * KIB
            SGA = [S.alloc(o + i * 2 * KIB, 2 * KIB) for i in range(2)]; o += 4 * KIB
            SGR = [S.alloc(o + i * 2 * KIB, 2 * KIB) for i in range(2)]; o += 4 * KIB
            MXA = [S.alloc(o + i * 2 * KIB, 2 * KIB) for i in range(2)]; o += 4 * KIB
            MXR = [S.alloc(o + i * 2 * KIB, 2 * KIB) for i in range(2)]; o += 4 * KIB
            it = 0
            for mg in range(2):
                for mm in range(4):
                    m = mg * 4 + mm
                    wga_t, wga = wload(w_in, 0, D, Z_GA + m * 128, 128)
                    wao_t, wao = wload(w_att_o, 0, 512, m * 128, 128)
                    wgr_t, wgr = wload(w_in, 0, D, Z_GR + m * 128, 128)
                    wro_t, wro = wload(w_rec_o, 0, D, m * 128, 128)
                    for tb in range(NTB):
                        sl = slice(tb * TB, (tb + 1) * TB)
                        sga, sgr, mxa, mxr = SGA[it % 2], SGR[it % 2], MXA[it % 2], MXR[it % 2]
                        it += 1
                        ps = next_ps()
                        mm_group(ps, ps.ap[:, :], proj_fm(wga, 0, HT, NCH, tb), reads=HT + wga_t)
                        S.op("act", lambda e, ps=ps, sga=sga, m=m: e.activation(
                            out=sga.ap[:, :], in_=ps.ap[:, :], func=AF.Sigmoid, bias=par(P_BIN + Z_GA // 128 + m)),
                            reads=[ps, PAR], writes=[sga])
                        ps = next_ps()
                        mm_group(ps, ps.ap[:, :], proj_fm(wao, 0, ATT, 4, tb), reads=ATT + wao_t)
                        S.op("dve", lambda e, ps=ps, sga=sga, mxa=mxa: e.tensor_tensor(
                            out=mxa.ap[:, :], in0=ps.ap[:, :], in1=sga.ap[:, :], op=ALU.mult),
                            reads=[ps, sga], writes=[mxa])
                        ps = next_ps()
                        mm_group(ps, ps.ap[:, :], proj_fm(wgr, 0, HT, NCH, tb), reads=HT + wgr_t)
                        S.op("act", lambda e, ps=ps, sgr=sgr, m=m: e.activation(
                            out=sgr.ap[:, :], in_=ps.ap[:, :], func=AF.Sigmoid, bias=par(P_BIN + Z_GR // 128 + m)),
                            reads=[ps, PAR], writes=[sgr])
                        ps = next_ps()
                        mm_group(ps, ps.ap[:, :], proj_fm(wro, 0, RECT, NCH, tb), reads=RECT + wro_t)
                        S.op("dve", lambda e, ps=ps, sgr=sgr, mxr=mxr: e.tensor_tensor(
                            out=mxr.ap[:, :], in0=ps.ap[:, :], in1=sgr.ap[:, :], op=ALU.mult),
                            reads=[ps, sgr], writes=[mxr])
                        S.op("dve", lambda e, mxa=mxa, mxr=mxr, m=m, sl=sl: e.tensor_tensor(
                            out=MIX[m].ap[:, sl], in0=mxa.ap[:, :], in1=mxr.ap[:, :], op=ALU.add),
                            reads=[mxa, mxr], writes=[MIX[m]])

            checkpoint(3, MIX)
            XT = [S.alloc(HT_OFF + c * 8 * KIB, 8 * KIB) for c in range(NCH)]
            for c in range(NCH):
                sp_dma(xl_sems[c], [(XT[c].ap[:, :], xT[c * 128:(c + 1) * 128, :])], writes=[XT[c]])
            H2 = [S.alloc(SCR + 32 * KIB + c * 4 * KIB, 4 * KIB, BF16) for c in range(NCH)]
            RSTD1 = S.alloc(ATT_OFF + 8 * KIB, 8 * KIB)
            SQF1 = [S.alloc(SCR + 64 * KIB + i * 8 * KIB, 8 * KIB) for i in range(2)]
            SSQ1 = S.alloc(SCR + 80 * KIB, 8 * KIB)
            SSB1 = S.alloc(SCR + 88 * KIB, 4 * KIB, BF16)
            for mg in range(2):
                for mm in range(4):
                    m = mg * 4 + mm
                    wo_t, wo = wload(w_out, 0, D, m * 128, 128)
                    for tb in range(NTB):
                        sl = slice(tb * TB, (tb + 1) * TB)
                        ps = next_ps()
                        mm_group(ps, ps.ap[:, :], proj_fm(wo, 0, MIX, NCH, tb), reads=MIX + wo_t)
                        S.op("dve", lambda e, ps=ps, m=m, sl=sl: e.tensor_tensor(
                            out=XT[m].ap[:, sl], in0=ps.ap[:, :], in1=XT[m].ap[:, sl], op=ALU.add),
                            reads=[ps, XT[m]], writes=[XT[m]])
                    if m >= 1:
                        stats_acc(XT[m - 1], SQF1[(m - 1) % 2], SSQ1, m - 1 == 0)
                    S.op("act", lambda e, m=m: e.activation(out=H2[m].ap[:, :], in_=XT[m].ap[:, :], func=AF.Copy,
                                                            scale=par(P_G2 + m)),
                         reads=[XT[m], PAR], writes=[H2[m]])
            stats_acc(XT[NCH - 1], SQF1[(NCH - 1) % 2], SSQ1, False)
            stats_acc_finish(SSQ1, SSB1, RSTD1)

            checkpoint(4, XT)
            checkpoint(5, H2)
            RT = [S.alloc(SCR + 64 * KIB + i * 2 * KIB, 2 * KIB) for i in range(2)]
            a_offs = [SCR + i * 2 * KIB for i in range(16)]
            a_offs += [SCR + 68 * KIB + i * 2 * KIB for i in range(12)]
            a_offs += [ATT_OFF + i * 2 * KIB for i in range(4)]
            assert len(a_offs) == 32 and SCR + 68 * KIB + 24 * KIB <= ARENA_BYTES
            osems = [S.dma_sem("os%d" % i) for i in range(2)]
            AT = None
            it = 0
            for hf in range(2):
                AT = [S.alloc(a_offs[jn], 2 * KIB, BF16) for jn in range(32)]
                for jg in range(8):
                    for jj in range(4):
                        jn = jg * 4 + jj
                        w1_t, w1 = wload(w_ff1, 0, D, jn * 128, 128)
                        for tl in range(2):
                            tb = hf * 2 + tl
                            sl = slice(tb * TB, (tb + 1) * TB)
                            ps = next_ps()
                            mm_group(ps, ps.ap[:, :], proj_fm(w1, 0, H2, NCH, tb), reads=H2 + w1_t)
                            rt = RT[it % 2]
                            it += 1
                            S.op("act", lambda e, ps=ps, rt=rt: e.activation(out=rt.ap[:, :], in_=ps.ap[:, :], func=AF.Relu),
                                 reads=[ps], writes=[rt])
                            S.op("dve", lambda e, rt=rt, sl=sl: e.tensor_tensor(
                                out=rt.ap[:, :], in0=rt.ap[:, :], in1=RSTD1.ap[:, sl], op=ALU.mult),
                                reads=[rt, RSTD1], writes=[rt])
                            S.op("dve", lambda e, rt=rt, jn=jn, tl=tl, AT=AT: e.tensor_tensor(
                                out=AT[jn].ap[:, tl * TB:(tl + 1) * TB], in0=rt.ap[:, :], in1=rt.ap[:, :], op=ALU.mult),
                                reads=[rt], writes=[AT[jn]])
                if hf == 1:
                    SQF2 = [S.alloc(SCR + 32 * KIB + i * 8 * KIB, 8 * KIB) for i in range(2)]
                    SSQ2 = S.alloc(SCR + 48 * KIB, 8 * KIB)
                    SSB2 = S.alloc(SCR + 56 * KIB, 4 * KIB, BF16)
                for m in range(NCH):
                    w2_t, w2 = wload(w_ff2, 0, 4 * D, m * 128, 128)
                    for tl in range(2):
                        tb = hf * 2 + tl
                        sl = slice(tb * TB, (tb + 1) * TB)
                        ps = next_ps()
                        terms = [(w2[:, jn, :], AT[jn].ap[:, tl * TB:(tl + 1) * TB]) for jn in range(32)]
                        mm_group(ps, ps.ap[:, :], terms, reads=AT + w2_t)
                        S.op("dve", lambda e, ps=ps, m=m, sl=sl: e.tensor_tensor(
                            out=XT[m].ap[:, sl], in0=ps.ap[:, :], in1=XT[m].ap[:, sl], op=ALU.add),
                            reads=[ps, XT[m]], writes=[XT[m]])
                    if hf == 1 and m >= 1:
                        stats_acc(XT[m - 1], SQF2[(m - 1) % 2], SSQ2, m - 1 == 0)
                if hf == 1:
                    stats_acc(XT[NCH - 1], SQF2[(NCH - 1) % 2], SSQ2, False)

            checkpoint(6, XT)
            o = SCR
            RSTD2 = S.alloc(o, 8 * KIB); o += 8 * KIB
            OUT = [S.alloc(o + i * 8 * KIB, 8 * KIB) for i in range(2)]; o += 16 * KIB
            stats_acc_finish(SSQ2, SSB2, RSTD2)
            for c in range(NCH):
                ot = OUT[c % 2]
                S.op("dve", lambda e, c=c, ot=ot: e.scalar_tensor_tensor(out=ot.ap[:, :], in0=XT[c].ap[:, :],
                                                                         scalar=par(P_GF + c), in1=RSTD2.ap[:, :],
                                                                         op0=ALU.mult, op1=ALU.mult),
                     reads=[XT[c], PAR, RSTD2], writes=[ot])
                sp_dma(osems[c % 2], [(outT[c * 128:(c + 1) * 128, :], ot.ap[:, :])], reads=[ot])
            S.final_waits = [(ds.sem, ds.count) for ds in osems]

        except _Stop:
            pass

        with nc.Block() as block:
            @block.tensor
            def _(e):
                S.emit("pe", e)

            @block.scalar
            def _(e):
                S.emit("act", e)

            @block.vector
            def _(e):
                S.emit("dve", e)

            @block.gpsimd
            def _(e):
                S.emit("pool", e)

            @block.sync
            def _(e):
                S.emit("sp", e)
    return nc


def _cols(v, n):
    return np.ascontiguousarray(np.asarray(v, np.float32).reshape(n, 128).T)


def _build_t2(rpb):
    rpb = np.asarray(rpb, np.float32)
    kc = np.arange(64)[:, None]
    qc = np.arange(64)[None, :]
    dcol = np.clip(kc - qc, -15, 15) + 15
    ws = np.clip(qc - 8, 0, 48)
    valid = (kc >= ws) & (kc < ws + 16)
    t2 = np.empty((4, 2, 64, 2, 14, 64), np.float32)
    for a in range(2):
        for e in range(14):
            g = rpb[:, e + a][:, dcol]
            g = np.where(valid[None], g, np.float32(NEG))
            t2[:, a, :, :, e, :] = g.reshape(4, 2, 64, 64).transpose(0, 2, 1, 3)
    return np.ascontiguousarray(t2.reshape(4, 128, 2 * 14 * 64))


_NC_CACHE = {}


def kernel(x, ln1_g, w_in, b_in, rpb, w_att_o, conv_w, conv_b, w_rg_a, b_rg_a, w_rg_i, b_rg_i,
           lru_lambda, w_rec_o, w_out, ln2_g, w_ff1, w_ff2, lnf_g):
    f = lambda a: np.ascontiguousarray(np.asarray(a, np.float32))
    x = f(x)
    B = x.shape[0]
    params = np.zeros((128, NPAR), np.float32)
    params[:, P_BIN:P_BIN + 44] = _cols(f(b_in)[0], 44)
    params[:, P_G1:P_G1 + 8] = _cols(f(ln1_g)[0], 8)
    params[:, P_G2:P_G2 + 8] = _cols(f(ln2_g)[0], 8)
    params[:, P_GF:P_GF + 8] = _cols(f(lnf_g), 8)
    cw = f(conv_w)[0]
    for j in range(4):
        params[:, P_CW + j * 8:P_CW + (j + 1) * 8] = _cols(cw[j], 8)
    params[:, P_CB:P_CB + 8] = _cols(f(conv_b)[0], 8)
    for d in range(2):
        params[:, P_BA + d * 8:P_BA + (d + 1) * 8] = _cols(f(b_rg_a)[0, d], 8)
        params[:, P_BI + d * 8:P_BI + (d + 1) * 8] = _cols(f(b_rg_i)[0, d], 8)
        params[:, P_LAM + d * 8:P_LAM + (d + 1) * 8] = _cols(f(lru_lambda)[0, d], 8)
    wa = f(w_rg_a)[0]
    wi = f(w_rg_i)[0]
    w_rg = np.ascontiguousarray(np.stack([wa[0], wi[0], wa[1], wi[1]], axis=0))
    t2 = _build_t2(f(rpb)[0])
    shared = {
        "w_in": f(w_in)[0], "w_att_o": f(w_att_o)[0], "w_rec_o": f(w_rec_o)[0], "w_out": f(w_out)[0],
        "w_ff1": f(w_ff1)[0], "w_ff2": f(w_ff2)[0], "w_rg": w_rg, "params": params, "t2": t2,
        "ident": np.eye(128, dtype=np.float32),
    }
    in_maps = []
    for b in range(B):
        m = dict(shared)
        m["xT"] = np.ascontiguousarray(x[b].T)
        in_maps.append(m)
    if "nc" not in _NC_CACHE:
        _NC_CACHE["nc"] = build_program()
    nc = _NC_CACHE["nc"]
    res = run_bass_kernel_spmd(nc, in_maps, core_ids=list(range(B)))
    out = np.stack([np.ascontiguousarray(r["outT"].T) for r in res.results], axis=0)
    return out.astype(np.float32)
```

```python
import numpy as np
from contextlib import ExitStack

import concourse.bass as bass
import concourse.mybir as mybir
from concourse.bass_utils import run_bass_kernel_spmd
from concourse.ap import AP

F32 = mybir.dt.float32
BF16 = mybir.dt.bfloat16
AF = mybir.ActivationFunctionType
ALU = mybir.AluOpType

T = 2048
D = 1024
NCH = 8
TB = 512
NTB = T // TB
D_IN = 5632
EPS = 1e-6
GRID_W = 64
ROWS = T // GRID_W
NEG = -30000.0

P_BIN = 0
P_G1 = 44
P_G2 = 52
P_GF = 60
P_CW = 68
P_CB = 100
P_BA = 108
P_BI = 124
P_LAM = 140
NPAR = 156

Z_Q, Z_K, Z_V, Z_U, Z_Y, Z_GA, Z_GR = 0, 512, 1024, 1536, 2560, 3584, 4608

KIB = 1024
ARENA_BYTES = 207 * KIB


class Tile:
    __slots__ = ("ap", "lo", "hi", "wr", "rd")

    def __init__(self, ap, lo, hi):
        self.ap = ap
        self.lo = lo
        self.hi = hi
        self.wr = {}
        self.rd = {}


class DmaSem:
    def __init__(self, sem):
        self.sem = sem
        self.count = 0


class Sched:
    ENG = ("pe", "act", "dve", "pool", "sp")

    def __init__(self, nc, es):
        self.nc = nc
        self.es = es
        self.sem = {n: es.enter_context(nc.semaphore("s_" + n)) for n in self.ENG}
        self.count = {n: 0 for n in self.ENG}
        self.ops = {n: [] for n in self.ENG}
        self.tiles = []
        self.arena = None
        self.n_dsem = 0
        self.final_waits = []

    def set_arena(self, arena):
        self.arena = arena

    def alloc(self, lo, nbytes, dtype=F32, parts=128):
        assert lo % 4 == 0 and nbytes % 4 == 0
        hi = lo + nbytes
        assert hi <= ARENA_BYTES, (lo, nbytes)
        ap = self.arena[0:parts, lo // 4:hi // 4]
        if dtype == BF16:
            ap = ap.bitcast(BF16)
        t = Tile(ap, lo, hi)
        for o in self.tiles:
            if o.lo < hi and lo < o.hi:
                for src in (o.wr, o.rd):
                    for s, v in src.items():
                        if t.wr.get(s, (None, 0))[1] < v[1]:
                            t.wr[s] = v
        self.tiles.append(t)
        return t

    def extern_tile(self, ap):
        return Tile(ap, -1, -1)

    def dma_sem(self, name=None):
        self.n_dsem += 1
        s = self.es.enter_context(self.nc.semaphore(name or ("d%d" % self.n_dsem)))
        return DmaSem(s)

    def _deps(self, eng, reads, writes):
        own = id(self.sem[eng])
        deps = {}

        def add(k, v):
            if deps.get(k, (None, 0))[1] < v[1]:
                deps[k] = v

        for t in reads:
            for k, v in t.wr.items():
                add(k, v)
        for t in writes:
            for k, v in t.wr.items():
                add(k, v)
            for k, v in t.rd.items():
                add(k, v)
        if eng == "pe":
            deps.pop(own, None)
        return list(deps.values())

    def op(self, eng, fn, reads=(), writes=()):
        deps = self._deps(eng, reads, writes)
        self.count[eng] += 1
        c = self.count[eng]
        sem = self.sem[eng]
        self.ops[eng].append((fn, deps, (sem, 1)))
        k = id(sem)
        for t in reads:
            t.rd[k] = (sem, c)
        for t in writes:
            t.wr[k] = (sem, c)

    def dma(self, queue, dsem, pairs, reads=(), writes=(), extra_deps=()):
        deps = self._deps(queue, reads, writes) + list(extra_deps)
        n = len(pairs)
        dsem.count += 16 * n
        c = dsem.count
        sem = dsem.sem

        def fn(e, pairs=pairs, sem=sem):
            for o, i in pairs:
                e.dma_start(out=o, in_=i).then_inc(sem, 16)
            return None

        self.ops[queue].append((fn, deps, None))
        k = id(sem)
        for t in reads:
            t.rd[k] = (sem, c)
        for t in writes:
            t.wr[k] = (sem, c)

    def emit(self, eng, e):
        seen = {}
        for fn, deps, sig in self.ops[eng]:
            for sem, v in deps:
                if seen.get(id(sem), 0) < v:
                    e.wait_ge(sem, v)
                    seen[id(sem)] = v
            ins = fn(e)
            if sig is not None:
                ins.then_inc(sig[0], sig[1])
        if eng == "sp":
            for sem, v in self.final_waits:
                e.wait_ge(sem, v)


class _Stop(Exception):
    pass


def build_program(stop=None):
    nc = bass.Bass("TRN2", target_bir_lowering=False)
    es = ExitStack()
    with es:
        es.enter_context(nc.allow_low_precision("bf16 matmul operands, fp32 accumulation"))
        xT = nc.dram_tensor("xT", [D, T], F32, kind="ExternalInput").ap()
        w_in = nc.dram_tensor("w_in", [D, D_IN], F32, kind="ExternalInput").ap()
        w_att_o = nc.dram_tensor("w_att_o", [512, D], F32, kind="ExternalInput").ap()
        w_rec_o = nc.dram_tensor("w_rec_o", [D, D], F32, kind="ExternalInput").ap()
        w_out = nc.dram_tensor("w_out", [D, D], F32, kind="ExternalInput").ap()
        w_ff1 = nc.dram_tensor("w_ff1", [D, 4 * D], F32, kind="ExternalInput").ap()
        w_ff2 = nc.dram_tensor("w_ff2", [4 * D, D], F32, kind="ExternalInput").ap()
        w_rg = nc.dram_tensor("w_rg", [4, 16, 64, 64], F32, kind="ExternalInput").ap()
        params = nc.dram_tensor("params", [128, NPAR], F32, kind="ExternalInput").ap()
        t2d = nc.dram_tensor("t2", [4, 128, 2 * 14 * 64], F32, kind="ExternalInput").ap()
        identd = nc.dram_tensor("ident", [128, 128], F32, kind="ExternalInput").ap()
        outT = nc.dram_tensor("outT", [D, T], F32, kind="ExternalOutput").ap()

        arena = es.enter_context(nc.sbuf_tensor("arena", [128, ARENA_BYTES // 4], F32))
        S = Sched(nc, es)
        S.set_arena(arena)
        psum = []
        psall = es.enter_context(nc.psum_tensor("psall", [128, 8 * TB], F32))
        for i in range(8):
            psum.append(Tile(psall[:, i * TB:(i + 1) * TB], -1, -1))

        def pair_ap(i):
            return psall[:, 2 * i * TB:(2 * i + 2) * TB]
        ps_rr = [0]

        ps_excl = set()

        def next_ps():
            while True:
                i = ps_rr[0] % 8
                ps_rr[0] += 1
                if i not in ps_excl:
                    return psum[i]

        def next_pair():
            while True:
                i = ps_rr[0] % 8
                if i % 2:
                    ps_rr[0] += 1
                    continue
                ps_rr[0] += 2
                if i not in ps_excl and (i + 1) not in ps_excl:
                    return psum[i], psum[i + 1], pair_ap(i // 2)

        off = 0
        PAR = S.alloc(off, NPAR * 4); off += NPAR * 4
        ONES = S.alloc(off, 128 * 2, BF16); off += 256
        CL = S.alloc(off, 16 * 4); off += 64
        CL2 = S.alloc(off, 16 * 4); off += 64
        LTMP = S.alloc(off, 16 * 4); off += 64
        off = 2 * KIB
        NSUB = 16
        SUB = 2 * KIB
        wsub = [S.alloc(off + i * SUB, SUB, BF16) for i in range(NSUB)]
        wsems = [S.dma_sem("wsem%d" % i) for i in range(NSUB)]
        w_ptr = [0]
        w_hist = []
        W_DEPTH = 4

        sp_hist = []
        SP_DEPTH = 4

        def sp_dma(dsem, pairs, reads=(), writes=()):
            extra = [sp_hist[-SP_DEPTH]] if len(sp_hist) >= SP_DEPTH else []
            S.dma("sp", dsem, pairs, reads=reads, writes=writes, extra_deps=extra)
            sp_hist.append((dsem.sem, dsem.count))

        def pool_dma(dsem, pairs, writes):
            extra = [w_hist[-W_DEPTH]] if len(w_hist) >= W_DEPTH else []
            S.dma("pool", dsem, pairs, writes=writes, extra_deps=extra)
            w_hist.append((dsem.sem, dsem.count))
        off += NSUB * SUB
        HT_OFF = off
        off += 32 * KIB
        REC_OFF = off
        off += 32 * KIB
        ATT_OFF = off
        off += 16 * KIB
        SCR = off
        assert SCR == 114 * KIB

        def par(col, n=1):
            return PAR.ap[:, col:col + n]

        def wload(w2d, r0, nr, c0, ncols):
            nk = nr // 128
            nbytes = nk * ncols * 2
            k = 1 if nbytes <= SUB else (2 if nbytes <= 2 * SUB else 4)
            assert nbytes <= k * SUB
            p = w_ptr[0]
            if p % k:
                p += k - p % k
            i0 = p % NSUB
            w_ptr[0] = p + k
            tiles = wsub[i0:i0 + k]
            lo = wsub[i0].lo
            ap = S.arena[:, lo // 4:(lo + k * SUB) // 4].bitcast(BF16)
            view = ap[:, 0:nk * ncols].rearrange("p (k c) -> p k c", k=nk)
            src = w2d[r0:r0 + nr, c0:c0 + ncols].rearrange("(k p) c -> p k c", p=128)
            pool_dma(wsems[i0], [(view, src)], tiles)
            return tiles, view

        def mm_group(out_tile, out_ap, terms, reads):
            def fn(e, terms=terms, out_ap=out_ap):
                n = len(terms)
                ins = None
                for i, (l, r) in enumerate(terms):
                    ins = e.matmul(out_ap, l, r, start=(i == 0), stop=(i == n - 1))
                return ins
            S.op("pe", fn, reads=reads, writes=[out_tile])

        csem = S.dma_sem("csem")
        S.dma("sp", csem, [(PAR.ap[:, :], params[:, :])], writes=[PAR])
        S.op("dve", lambda e: e.memset(ONES.ap[:, :], 1.0), writes=[ONES])
        EPSC = S.alloc(1600, 4)
        ONEC = S.alloc(1604, 4)
        S.op("dve", lambda e: e.memset(EPSC.ap[:, :], EPS), writes=[EPSC])
        S.op("dve", lambda e: e.memset(ONEC.ap[:, :], 1.0), writes=[ONEC])
        S.op("act", lambda e: e.activation(out=LTMP.ap[:, :], in_=par(P_LAM, 16), func=AF.Exp, scale=-1.0),
             reads=[PAR], writes=[LTMP])
        S.op("act", lambda e: e.activation(out=LTMP.ap[:, :], in_=LTMP.ap[:, :], func=AF.Ln, bias=ONEC.ap[:, 0:1]),
             reads=[LTMP, ONEC], writes=[LTMP])
        S.op("dve", lambda e: e.tensor_scalar(out=CL.ap[:, :], in0=LTMP.ap[:, :], scalar1=-8.0, scalar2=None,
                                              op0=ALU.mult), reads=[LTMP], writes=[CL])
        S.op("dve", lambda e: e.tensor_scalar(out=CL2.ap[:, :], in0=LTMP.ap[:, :], scalar1=-16.0, scalar2=None,
                                              op0=ALU.mult), reads=[LTMP], writes=[CL2])
        IDENT = S.alloc(1100, 256, BF16)
        HB = S.alloc(1360, 128)
        CLH = S.alloc(1488, 64)
        QTR = S.alloc(1552, 4)
        isem = S.dma_sem("isem")
        pool_dma(isem, [(IDENT.ap[:, :], identd[:, :])], [IDENT])
        S.op("dve", lambda e: e.memset(QTR.ap[:, :], 0.25), writes=[QTR])
        S.op("dve", lambda e: e.tensor_scalar(out=HB.ap[:, :], in0=par(P_BA, 32), scalar1=0.5, scalar2=None,
                                              op0=ALU.mult), reads=[PAR], writes=[HB])
        S.op("dve", lambda e: e.tensor_scalar(out=CLH.ap[:, :], in0=LTMP.ap[:, :], scalar1=-4.0, scalar2=None,
                                              op0=ALU.mult), reads=[LTMP], writes=[CLH])

        xl_sems = [S.dma_sem("xl%d" % c) for c in range(NCH)]

        def stats_chunk(xt, sq, banks, first, last, eng="act"):
            if eng == "act":
                S.op("act", lambda e: e.activation(out=sq.ap[:, :], in_=xt.ap[:, :], func=AF.Square),
                     reads=[xt], writes=[sq])
            else:
                S.op("dve", lambda e: e.tensor_tensor(out=sq.ap[:, :], in0=xt.ap[:, :], in1=xt.ap[:, :], op=ALU.mult),
                     reads=[xt], writes=[sq])
            for tb in range(NTB):
                def fn(e, tb=tb):
                    return e.matmul(banks[tb].ap[:, :], ONES.ap[:, :], sq.ap[:, tb * TB:(tb + 1) * TB],
                                    start=first, stop=last)
                S.op("pe", fn, reads=[ONES, sq], writes=[banks[tb]])

        def stats_finish(RSTD, banks):
            for tb in range(NTB):
                sl = slice(tb * TB, (tb + 1) * TB)
                S.op("act", lambda e, tb=tb, sl=sl: e.activation(out=RSTD.ap[:, sl], in_=banks[tb].ap[:, :],
                                                                  func=AF.Ln, scale=1.0 / D, bias=EPSC.ap[:, 0:1]),
                     reads=[banks[tb], EPSC], writes=[RSTD])
            S.op("act", lambda e: e.activation(out=RSTD.ap[:, :], in_=RSTD.ap[:, :], func=AF.Exp, scale=-0.5),
                 reads=[RSTD], writes=[RSTD])

        def stats_acc(xt, sqf, ssq, first):
            S.op("act", lambda e: e.activation(out=sqf.ap[:, :], in_=xt.ap[:, :], func=AF.Square),
                 reads=[xt], writes=[sqf])
            if first:
                S.op("dve", lambda e: e.tensor_copy(out=ssq.ap[:, :], in_=sqf.ap[:, :]), reads=[sqf], writes=[ssq])
            else:
                S.op("dve", lambda e: e.tensor_tensor(out=ssq.ap[:, :], in0=ssq.ap[:, :], in1=sqf.ap[:, :], op=ALU.add),
                     reads=[ssq, sqf], writes=[ssq])

        def stats_acc_finish(ssq, ssb, RSTD):
            S.op("dve", lambda e: e.tensor_copy(out=ssb.ap[:, :], in_=ssq.ap[:, :]), reads=[ssq], writes=[ssb])
            banks = [next_ps() for _ in range(NTB)]
            for tb in range(NTB):
                def fn(e, tb=tb):
                    return e.matmul(banks[tb].ap[:, :], ONES.ap[:, :], ssb.ap[:, tb * TB:(tb + 1) * TB],
                                    start=True, stop=True)
                S.op("pe", fn, reads=[ONES, ssb], writes=[banks[tb]])
            stats_finish(RSTD, banks)

        def rms_stats(XT, sq_tiles, RSTD):
            banks = [next_ps() for _ in range(NTB)]
            for c in range(NCH):
                stats_chunk(XT[c], sq_tiles[c % len(sq_tiles)], banks, c == 0, c == NCH - 1,
                            eng=("act" if c % 2 == 0 else "dve"))
            stats_finish(RSTD, banks)


        dsem_dbg = S.dma_sem("dbg")

        def checkpoint(k, tiles):
            if stop != k:
                return
            for i, t in enumerate(tiles):
                S.dma("pool", dsem_dbg, [(outT[i * 128:(i + 1) * 128, :], t.ap[:, 0:T])], reads=[t])
            S.final_waits = [(dsem_dbg.sem, dsem_dbg.count)]
            raise _Stop()

        try:
            HT = [S.alloc(HT_OFF + c * 4 * KIB, 4 * KIB, BF16) for c in range(NCH)]
            XT0 = [S.alloc(SCR + c * 8 * KIB, 8 * KIB) for c in range(NCH)]
            SQ0 = [S.alloc(SCR + 64 * KIB + i * 4 * KIB, 4 * KIB, BF16) for i in range(2)]
            RSTD0 = S.alloc(SCR + 72 * KIB, 8 * KIB)
            for c in range(NCH):
                sp_dma(xl_sems[c], [(XT0[c].ap[:, :], xT[c * 128:(c + 1) * 128, :])], writes=[XT0[c]])
            rms_stats(XT0, SQ0, RSTD0)
            HTS = [[Tile(HT[c].ap[:, tb * TB:(tb + 1) * TB], -1, -1) for tb in range(NTB)] for c in range(NCH)]
            for tb in range(NTB):
                for c in range(NCH):
                    sl = slice(tb * TB, (tb + 1) * TB)
                    S.op("dve", lambda e, c=c, sl=sl: e.scalar_tensor_tensor(
                        out=HT[c].ap[:, sl], in0=XT0[c].ap[:, sl], scalar=par(P_G1 + c), in1=RSTD0.ap[:, sl],
                        op0=ALU.mult, op1=ALU.mult), reads=[XT0[c], PAR, RSTD0], writes=[HT[c], HTS[c][tb]])

            def proj_fm(wview, ncol_lo, k_tiles, nk, tb):
                sl = slice(tb * TB, (tb + 1) * TB)
                return [(wview[:, k, ncol_lo:ncol_lo + 128], k_tiles[k].ap[:, sl]) for k in range(nk)]

            checkpoint(0, HT)
            RECT = [S.alloc(REC_OFF + c * 4 * KIB, 4 * KIB, BF16) for c in range(NCH)]
            WG = S.alloc(ATT_OFF, 8 * KIB, BF16)
            wgv = WG.ap[:, :].rearrange("p (g c m) -> p g c m", g=4, c=8)
            S.op("pool", lambda e: e.memset(WG.ap[:, :], 0.0), writes=[WG])
            wgsem = S.dma_sem("wgsem")
            for g in range(4):
                pairs = []
                for a in range(2):
                    src = w_rg[g, a:16:2, :, :].rearrange("c p d -> p c d")
                    dst = wgv[a * 64:(a + 1) * 64, g, :, a * 64:(a + 1) * 64]
                    pairs.append((dst, src))
                pool_dma(wgsem, pairs, [WG])
            DGC = [S.alloc(ATT_OFF + 12 * KIB + i * KIB, KIB, BF16) for i in range(2)]
            o = SCR
            UW = 2052
            UB_ = [S.alloc(o + i * UW * 2, UW * 2, BF16) for i in range(2)]; o += 2 * UW * 2
            GY_ = [S.alloc(o + i * 4 * KIB, 4 * KIB, BF16) for i in range(3)]; o += 12 * KIB
            UC_ = [S.alloc(o + i * 8 * KIB, 8 * KIB) for i in range(2)]; o += 16 * KIB
            UCB_ = [S.alloc(ATT_OFF + 8 * KIB, 4 * KIB, BF16), S.alloc(o, 4 * KIB, BF16)]; o += 4 * KIB
            NSET = 3
            HALF = T // 2
            SA = [S.alloc(o + i * 4 * KIB, 4 * KIB) for i in range(NSET)]; o += NSET * 4 * KIB
            SM = [S.alloc(o + i * 4 * KIB, 4 * KIB) for i in range(NSET)]; o += NSET * 4 * KIB
            ST = [S.alloc(o + i * 4 * KIB, 4 * KIB) for i in range(NSET)]; o += NSET * 4 * KIB
            HH = [[S.alloc(o + (d * 2 + h) * 4 * KIB, 4 * KIB) for h in range(2)] for d in range(2)]; o += 16 * KIB
            assert o <= ARENA_BYTES, o

            def rev(ap2d, n):
                pst = ap2d.ap[0][0]
                npart = ap2d.ap[0][1]
                return AP(ap2d.tensor, ap2d.offset + (n - 1), [[pst, npart], [-1, n]])

            lw = {}
            for c in range(NCH):
                lw[("u", c)] = wload(w_in, 0, D, Z_U + c * 128, 128)
                lw[("y", c)] = wload(w_in, 0, D, Z_Y + c * 128, 128)
            for i in range(2):
                S.op("pool", lambda e, i=i: e.memset(UB_[i].ap[:, :], 0.0), writes=[UB_[i]])

            def lru_s1(c):
                wu_t, wu = lw[("u", c)]
                wy_t, wy = lw[("y", c)]
                UB, GY = UB_[c % 2], GY_[c % 3]
                dg = DGC[c % 2]
                dgv = dg.ap[:, :].rearrange("p (j m) -> p j m", j=4)
                for j in range(4):
                    S.op("dve", lambda e, c=c, j=j, dgv=dgv: e.tensor_scalar(
                        out=dgv[:, j, :], in0=IDENT.ap[:, :], scalar1=par(P_CW + j * 8 + c), scalar2=None,
                        op0=ALU.mult), reads=[IDENT, PAR], writes=[dg])
                for hb in range(2):
                    pa, pb, pap = next_pair()
                    for tl, ps in ((0, pa), (1, pb)):
                        mm_group(ps, ps.ap[:, :], proj_fm(wu, 0, HT, NCH, 2 * hb + tl), reads=(([HTS[k][2 * hb + tl] for k in range(NCH)] if c == 0 else HT) + wu_t))
                    S.op("dve", lambda e, pap=pap, hb=hb, UB=UB, c=c: e.tensor_scalar(
                        out=UB.ap[:, 2 + hb * 2 * TB:2 + (hb + 1) * 2 * TB], in0=pap,
                        scalar1=par(P_BIN + Z_U // 128 + c), scalar2=None, op0=ALU.add),
                        reads=[pa, pb, PAR], writes=[UB])
                for hb in range(2):
                    pa, pb, pap = next_pair()
                    for tl, ps in ((0, pa), (1, pb)):
                        mm_group(ps, ps.ap[:, :], proj_fm(wy, 0, HT, NCH, 2 * hb + tl), reads=(([HTS[k][2 * hb + tl] for k in range(NCH)] if c == 0 else HT) + wy_t))
                    S.op("act", lambda e, pap=pap, hb=hb, GY=GY, c=c: e.activation(
                        out=GY.ap[:, hb * 2 * TB:(hb + 1) * 2 * TB], in_=pap, func=AF.Gelu_apprx_tanh,
                        bias=par(P_BIN + Z_Y // 128 + c)),
                        reads=[pa, pb, PAR], writes=[GY])

            def lru_s2(c):
                UB, UC, dg, UCB = UB_[c % 2], UC_[c % 2], DGC[c % 2], UCB_[c % 2]
                dgv = dg.ap[:, :].rearrange("p (j m) -> p j m", j=4)
                for hb in range(2):
                    pa, pb, pap = next_pair()
                    for tl, ps in ((0, pa), (1, pb)):
                        tb = 2 * hb + tl
                        terms = [(dgv[:, j, :], UB.ap[:, tb * TB + j:tb * TB + j + TB]) for j in range(4)]
                        mm_group(ps, ps.ap[:, :], terms, reads=[dg, UB])
                    S.op("dve", lambda e, pap=pap, hb=hb, UC=UC, c=c: e.tensor_scalar(
                        out=UC.ap[:, hb * 2 * TB:(hb + 1) * 2 * TB], in0=pap, scalar1=par(P_CB + c), scalar2=None,
                        op0=ALU.add), reads=[pa, pb, PAR], writes=[UC])
                S.op("dve", lambda e, UC=UC: e.tensor_copy(out=UCB.ap[:, :], in_=UC.ap[:, :]),
                     reads=[UC], writes=[UCB])

            ITEMS = [(0, 0), (1, 1), (0, 1), (1, 0)]
            item_ctr = [0]

            cur = {}

            def lru_s3(c, p):
                UCB = UCB_[c % 2]
                its = []
                for (d, h) in ITEMS[2 * p:2 * p + 2]:
                    s_ = item_ctr[0] % NSET
                    item_ctr[0] += 1
                    its.append((d, h, SA[s_], SM[s_], ST[s_]))
                cur[(c, p)] = its
                for (d, h, A, M, TI) in its:
                    for gi, G in ((0, A), (1, TI)):
                        g = d * 2 + gi
                        hbcol = gi * 16 + d * 8 + c
                        pa, pb, pap = next_pair()
                        for tl, ps in ((0, pa), (1, pb)):
                            tb = 2 * h + tl
                            mm_group(ps, ps.ap[:, :], [(wgv[:, g, c, :], UCB.ap[:, tb * TB:(tb + 1) * TB])], reads=[WG, UCB])
                        S.op("act", lambda e, pap=pap, G=G, hbcol=hbcol: e.activation(
                            out=G.ap[:, :], in_=pap, func=AF.Tanh, scale=0.5,
                            bias=HB.ap[:, hbcol:hbcol + 1]), reads=[pa, pb, HB], writes=[G])

            def lru_s4(c, p):
                its = cur[(c, p)]
                for (d, h, A, M, TI) in its:
                    ccol = d * 8 + c
                    S.op("act", lambda e, A=A, M=M, ccol=ccol: e.activation(
                        out=M.ap[:, :], in_=A.ap[:, :], func=AF.Exp,
                        scale=CL.ap[:, ccol:ccol + 1], bias=CL.ap[:, ccol:ccol + 1]),
                        reads=[A, CL], writes=[M])
                    S.op("act", lambda e, A=A, ccol=ccol: e.activation(
                        out=A.ap[:, :], in_=A.ap[:, :], func=AF.Exp,
                        scale=CLH.ap[:, ccol:ccol + 1], bias=CLH.ap[:, ccol:ccol + 1]),
                        reads=[A, CLH], writes=[A])
                for (d, h, A, M, TI) in its:
                    S.op("act", lambda e, M=M: e.activation(out=M.ap[:, :], in_=M.ap[:, :], func=AF.Sqrt,
                                                            scale=-0.25, bias=QTR.ap[:, 0:1]),
                         reads=[M, QTR], writes=[M])

            def lru_s5(c, p):
                UC = UC_[c % 2]
                its = cur[(c, p)]
                for (d, h, A, M, TI) in its:
                    hs = slice(h * HALF, (h + 1) * HALF)
                    S.op("dve", lambda e, TI=TI, UC=UC, hs=hs: e.scalar_tensor_tensor(
                        out=TI.ap[:, :], in0=TI.ap[:, :], scalar=1.0, in1=UC.ap[:, hs], op0=ALU.add, op1=ALU.mult),
                        reads=[TI, UC], writes=[TI])
                    S.op("dve", lambda e, TI=TI, M=M: e.tensor_tensor(out=TI.ap[:, :], in0=TI.ap[:, :], in1=M.ap[:, :],
                                                                      op=ALU.mult), reads=[TI, M], writes=[TI])
                    Hout = HH[d][h]
                    if d == 0:
                        if h == 0:
                            S.op("dve", lambda e, A=A, TI=TI, Hout=Hout: e.tensor_tensor_scan(
                                out=Hout.ap[:, :], data0=A.ap[:, :], data1=TI.ap[:, :], initial=0.0,
                                op0=ALU.mult, op1=ALU.add), reads=[A, TI], writes=[Hout])
                        else:
                            prev = HH[0][0]
                            S.op("dve", lambda e, A=A, TI=TI, Hout=Hout, prev=prev: e.tensor_tensor_scan(
                                out=Hout.ap[:, :], data0=A.ap[:, :], data1=TI.ap[:, :],
                                initial=prev.ap[:, HALF - 1:HALF], op0=ALU.mult, op1=ALU.add),
                                reads=[A, TI, prev], writes=[Hout])
                    else:
                        if h == 1:
                            S.op("dve", lambda e, A=A, TI=TI, Hout=Hout: e.tensor_tensor_scan(
                                out=rev(Hout.ap[:, :], HALF), data0=rev(A.ap[:, :], HALF), data1=rev(TI.ap[:, :], HALF),
                                initial=0.0, op0=ALU.mult, op1=ALU.add), reads=[A, TI], writes=[Hout])
                        else:
                            prev = HH[1][1]
                            S.op("dve", lambda e, A=A, TI=TI, Hout=Hout, prev=prev: e.tensor_tensor_scan(
                                out=rev(Hout.ap[:, :], HALF), data0=rev(A.ap[:, :], HALF), data1=rev(TI.ap[:, :], HALF),
                                initial=prev.ap[:, 0:1], op0=ALU.mult, op1=ALU.add),
                                reads=[A, TI, prev], writes=[Hout])

            def lru_s6(c, h):
                GY = GY_[c % 3]
                hs = slice(h * HALF, (h + 1) * HALF)
                S.op("dve", lambda e, h=h: e.tensor_tensor(out=HH[0][h].ap[:, :], in0=HH[0][h].ap[:, :],
                                                         in1=HH[1][h].ap[:, :], op=ALU.add),
                     reads=[HH[0][h], HH[1][h]], writes=[HH[0][h]])
                S.op("dve", lambda e, c=c, h=h, GY=GY, hs=hs: e.tensor_tensor(
                    out=RECT[c].ap[:, hs], in0=HH[0][h].ap[:, :], in1=GY.ap[:, hs], op=ALU.mult),
                    reads=[HH[0][h], GY], writes=[RECT[c]])

            lru_s1(0)
            lru_s2(0)
            for c in range(NCH):
                lru_s3(c, 0)
                lru_s4(c, 0)
                if c + 1 < NCH:
                    lru_s1(c + 1)
                lru_s5(c, 0)
                lru_s3(c, 1)
                if c + 1 < NCH:
                    lru_s2(c + 1)
                lru_s4(c, 1)
                lru_s5(c, 1)
                lru_s6(c, 1)
                lru_s6(c, 0)

            checkpoint(1, RECT)
            ATT = [S.alloc(ATT_OFF + j * 4 * KIB, 4 * KIB, BF16) for j in range(4)]
            o = SCR
            QT = [S.alloc(o + j * 4 * KIB, 4 * KIB, BF16) for j in range(4)]; o += 16 * KIB
            KT = [S.alloc(o + j * 4 * KIB, 4 * KIB, BF16) for j in range(4)]; o += 16 * KIB
            V1 = S.alloc(o, 16 * KIB, BF16); o += 16 * KIB
            V2 = S.alloc(o, 15 * KIB, BF16); o += 15 * KIB
            T2 = [S.alloc(o + i * 3584, 3584, BF16) for i in range(2)]; o += 7 * KIB
            PT = [S.alloc(o + i * 2 * KIB, 2 * KIB, BF16) for i in range(2)]; o += 4 * KIB
            NRM = [S.alloc(o + i * KIB, KIB) for i in range(2)]; o += 2 * KIB
            RDN = [S.alloc(o + i * KIB, KIB) for i in range(2)]; o += 2 * KIB
            assert o <= ARENA_BYTES
            t2sems = [S.dma_sem("t2s%d" % i) for i in range(2)]
            v1v = V1.ap[:, :].rearrange("p (b f) -> p b f", b=16)
            v2v = V2.ap[:, :].rearrange("p (b f) -> p b f", b=15)

            wv_t, wv = wload(w_in, 0, D, Z_V, 512)
            for (vv, Vt, nb, toff) in ((v1v, V1, 16, 0),):
                for b in range(nb):
                    t0 = toff + b * 128
                    ps = next_ps()
                    terms = [(HT[k].ap[:, t0:t0 + 128], wv[:, k, :]) for k in range(NCH)]
                    mm_group(ps, ps.ap[:, :], terms, reads=HT + wv_t)
                    if b % 2 == 0:
                        S.op("act", lambda e, ps=ps, vv=vv, b=b: e.activation(out=vv[:, b, :], in_=ps.ap[:, :], func=AF.Copy),
                             reads=[ps], writes=[Vt])
                    else:
                        S.op("dve", lambda e, ps=ps, vv=vv, b=b: e.tensor_copy(out=vv[:, b, :], in_=ps.ap[:, :]),
                             reads=[ps], writes=[Vt])

            v2sem = S.dma_sem("v2sem")
            sp_dma(v2sem, [(v2v[0:64, 0:15, :], v1v[64:128, 0:15, :]),
                           (v2v[64:128, 0:15, :], v1v[0:64, 1:16, :])], reads=[V1], writes=[V2])

            for j in range(4):
                wq_t, wq = wload(w_in, 0, D, Z_Q + j * 128, 128)
                wk_t, wk = wload(w_in, 0, D, Z_K + j * 128, 128)
                for tb in range(NTB):
                    sl = slice(tb * TB, (tb + 1) * TB)
                    ps = next_ps()
                    mm_group(ps, ps.ap[:, :], proj_fm(wq, 0, HT, NCH, tb), reads=HT + wq_t)
                    S.op("dve", lambda e, ps=ps, sl=sl, j=j: e.tensor_scalar(
                        out=QT[j].ap[:, sl], in0=ps.ap[:, :], scalar1=par(P_BIN + Z_Q // 128 + j), scalar2=0.125,
                        op0=ALU.add, op1=ALU.mult), reads=[ps, PAR], writes=[QT[j]])
                    ps = next_ps()
                    mm_group(ps, ps.ap[:, :], proj_fm(wk, 0, HT, NCH, tb), reads=HT + wk_t)
                    S.op("act", lambda e, ps=ps, sl=sl, j=j: e.activation(
                        out=KT[j].ap[:, sl], in_=ps.ap[:, :], func=AF.Identity, bias=par(P_BIN + Z_K // 128 + j)),
                        reads=[ps, PAR], writes=[KT[j]])

            units = [(j, qg, qp) for j in range(4) for qg in range(ROWS // 4) for qp in range(2)]
            ust = {}
            t2_state = {}

            def rec_S(ui):
                j, qg, qp = units[ui]
                if j not in t2_state:
                    t2t = T2[j % 2]
                    pool_dma(t2sems[j % 2], [(t2t.ap[:, :], t2d[j, :, :])], [t2t])
                    t2_state[j] = t2t
                qrs = [qg * 4 + qp * 2 + r for r in range(2)]
                rss = [min(max(qr - 4, 0), ROWS - 8) for qr in qrs]
                e0s = [rs - qr + 7 for rs, qr in zip(rss, qrs)]
                sbank = [psum[(2 * ui) % 4], psum[(2 * ui + 1) % 4]]
                ust[ui] = dict(qrs=qrs, rss=rss, e0s=e0s, sbank=sbank, pt=PT[ui % 2])
                t2t = t2_state[j]
                t2v = t2t.ap[:, :].rearrange("p (h e q) -> p h e q", h=2, e=14)

                def fn_s(e, j=j, qrs=qrs, rss=rss, sbank=sbank, e0s=e0s, t2v=t2v):
                    ins = None
                    same = (e0s[0] == e0s[1])
                    if same:
                        for hh in range(2):
                            tsl = t2v[:, hh, e0s[0]:e0s[0] + 7:2, :]
                            bap = AP(tsl.tensor, tsl.offset, [list(tsl.ap[0]), [0, 2], list(tsl.ap[1]), list(tsl.ap[2])])
                            ins = e.matmul(sbank[hh].ap[:, :], IDENT.ap[:, :], bap, start=True, stop=False)
                    for r in range(2):
                        if not same:
                            for hh in range(2):
                                ins = e.matmul(sbank[hh].ap[:, r * 256:(r + 1) * 256], IDENT.ap[:, :],
                                               t2v[:, hh, e0s[r]:e0s[r] + 7:2, :], start=True, stop=False)
                        for m in range(4):
                            for hh in range(2):
                                pl = slice(hh * 64, (hh + 1) * 64)
                                k0 = (rss[r] + 2 * m) * 64
                                col = (r * 4 + m) * 64
                                ins = e.matmul(sbank[hh].ap[:, col:col + 64], KT[j].ap[pl, k0:k0 + 128],
                                               QT[j].ap[pl, qrs[r] * 64:(qrs[r] + 1) * 64],
                                               start=False, stop=(m == 3 and (r == 1 or not same)))
                    return ins
                S.op("pe", fn_s, reads=[KT[j], QT[j], t2t, IDENT], writes=sbank)

            def rec_mid(ui):
                st = ust[ui]
                sbank, pt = st["sbank"], st["pt"]
                pap = pair_ap(ui % 2)
                S.op("act", lambda e, pap=pap, pt=pt: e.activation(out=pt.ap[:, :], in_=pap, func=AF.Exp),
                     reads=list(sbank), writes=[pt])

            nd_state = {}

            def rec_PV(ui):
                j, qg, qp = units[ui]
                st = ust[ui]
                rss, pt = st["rss"], st["pt"]
                if (j, qg) not in nd_state:
                    nd_state[(j, qg)] = psum[4 + (j * (ROWS // 4) + qg) % 3]
                nd = nd_state[(j, qg)]

                def fn_pv(e, j=j, qp=qp, rss=rss, pt=pt, nd=nd):
                    ins = None
                    ptv = pt.ap[:, :].rearrange("p (h r m q) -> p h r m q", h=2, r=2, m=4)
                    for r in range(2):
                        qi = qp * 2 + r
                        for m in range(4):
                            b = rss[r] + 2 * m
                            for hh in range(2):
                                pl = slice(hh * 64, (hh + 1) * 64)
                                h = 2 * j + hh
                                vsrc = v1v[:, b // 2, h * 64:(h + 1) * 64] if b % 2 == 0 else \
                                    v2v[:, (b - 1) // 2, h * 64:(h + 1) * 64]
                                ins = e.matmul(nd.ap[pl, qi * 64:(qi + 1) * 64], vsrc, ptv[:, hh, r, m, :],
                                               start=(m == 0), stop=(m == 3))
                    for m in range(4):
                        for hh in range(2):
                            pl = slice(hh * 64, (hh + 1) * 64)
                            ins = e.matmul(nd.ap[pl, 256 + qp * 128:256 + (qp + 1) * 128], ONES.ap[:, 0:64], ptv[:, hh, :, m, :],
                                           start=(m == 0), stop=(m == 3))
                    return ins
                S.op("pe", fn_pv, reads=[pt, V1, V2, ONES], writes=[nd])
                if qp == 1:
                    nrm = NRM[qg % 2]
                    rdn = RDN[qg % 2]
                    nnum = nd.ap[:, 0:256]
                    nden = nd.ap[:, 256:512]

                    def n_act1(nd=nd, nden=nden, rdn=rdn):
                        S.op("act", lambda e: e.activation(out=rdn.ap[:, :], in_=nden, func=AF.Ln),
                             reads=[nd], writes=[rdn])
                        S.op("act", lambda e: e.activation(out=rdn.ap[:, :], in_=rdn.ap[:, :], func=AF.Exp, scale=-1.0),
                             reads=[rdn], writes=[rdn])

                    def n_dve(nd=nd, nnum=nnum, nrm=nrm, rdn=rdn):
                        S.op("dve", lambda e: e.tensor_tensor(out=nrm.ap[:, :], in0=nnum, in1=rdn.ap[:, :], op=ALU.mult),
                             reads=[nd, rdn], writes=[nrm])

                    def n_act2(nrm=nrm, j=j, qg=qg):
                        S.op("act", lambda e: e.activation(
                            out=ATT[j].ap[:, qg * 256:(qg + 1) * 256], in_=nrm.ap[:, :], func=AF.Identity,
                            bias=par(P_BIN + Z_V // 128 + j)), reads=[nrm, PAR], writes=[ATT[j]])
                    pend_a1.append((ui + 2, n_act1))
                    pend_d.append((ui + 2, n_dve))
                    pend_a2.append((ui + 3, n_act2))

            pend_a1, pend_d, pend_a2 = [], [], []

            def flush(lst, ui):
                while lst and lst[0][0] <= ui:
                    lst.pop(0)[1]()

            rec_S(0)
            for ui in range(len(units)):
                if ui + 1 < len(units):
                    rec_S(ui + 1)
                flush(pend_a2, ui)
                flush(pend_a1, ui)
                rec_mid(ui)
                flush(pend_d, ui)
                rec_PV(ui)
            for lst in (pend_a1, pend_d, pend_a2):
                flush(lst, 10 ** 9)

            checkpoint(2, ATT + QT)
            o = SCR
            MIX = [S.alloc(o + m * 4 * KIB, 4 * KIB, BF16) for m in range(NCH)]; o += 32 * KIB
            SGA = [S.alloc(o + i * 2 * KIB, 2 * KIB) for i in range(2)]; o += 4 * KIB
            SGR = [S.alloc(o + i * 2 * KIB, 2 * KIB) for i in range(2)]; o += 4 * KIB
            MXA = [S.alloc(o + i * 2 * KIB, 2 * KIB) for i in range(2)]; o += 4 * KIB
            MXR = [S.alloc(o + i * 2 * KIB, 2 * KIB) for i in range(2)]; o += 4 * KIB
            it = 0
            for mg in range(2):
                for mm in range(4):
                    m = mg * 4 + mm
                    wga_t, wga = wload(w_in, 0, D, Z_GA + m * 128, 128)
                    wao_t, wao = wload(w_att_o, 0, 512, m * 128, 128)
                    wgr_t, wgr = wload(w_in, 0, D, Z_GR + m * 128, 128)
                    wro_t, wro = wload(w_rec_o, 0, D, m * 128, 128)
                    for tb in range(NTB):
                        sl = slice(tb * TB, (tb + 1) * TB)
                        sga, sgr, mxa, mxr = SGA[it % 2], SGR[it % 2], MXA[it % 2], MXR[it % 2]
                        it += 1
                        ps = next_ps()
                        mm_group(ps, ps.ap[:, :], proj_fm(wga, 0, HT, NCH, tb), reads=HT + wga_t)
                        S.op("act", lambda e, ps=ps, sga=sga, m=m: e.activation(
                            out=sga.ap[:, :], in_=ps.ap[:, :], func=AF.Sigmoid, bias=par(P_BIN + Z_GA // 128 + m)),
                            reads=[ps, PAR], writes=[sga])
                        ps = next_ps()
                        mm_group(ps, ps.ap[:, :], proj_fm(wao, 0, ATT, 4, tb), reads=ATT + wao_t)
                        S.op("dve", lambda e, ps=ps, sga=sga, mxa=mxa: e.tensor_tensor(
                            out=mxa.ap[:, :], in0=ps.ap[:, :], in1=sga.ap[:, :], op=ALU.mult),
                            reads=[ps, sga], writes=[mxa])
                        ps = next_ps()
                        mm_group(ps, ps.ap[:, :], proj_fm(wgr, 0, HT, NCH, tb), reads=HT + wgr_t)
                        S.op("act", lambda e, ps=ps, sgr=sgr, m=m: e.activation(
                            out=sgr.ap[:, :], in_=ps.ap[:, :], func=AF.Sigmoid, bias=par(P_BIN + Z_GR // 128 + m)),
                            reads=[ps, PAR], writes=[sgr])
                        ps = next_ps()
                        mm_group(ps, ps.ap[:, :], proj_fm(wro, 0, RECT, NCH, tb), reads=RECT + wro_t)
                        S.op("dve", lambda e, ps=ps, sgr=sgr, mxr=mxr: e.tensor_tensor(
                            out=mxr.ap[:, :], in0=ps.ap[:, :], in1=sgr.ap[:, :], op=ALU.mult),
                            reads=[ps, sgr], writes=[mxr])
                        S.op("dve", lambda e, mxa=mxa, mxr=mxr, m=m, sl=sl: e.tensor_tensor(
                            out=MIX[m].ap[:, sl], in0=mxa.ap[:, :], in1=mxr.ap[:, :], op=ALU.add),
                            reads=[mxa, mxr], writes=[MIX[m]])

            checkpoint(3, MIX)
            XT = [S.alloc(HT_OFF + c * 8 * KIB, 8 * KIB) for c in range(NCH)]
            for c in range(NCH):
                sp_dma(xl_sems[c], [(XT[c].ap[:, :], xT[c * 128:(c + 1) * 128, :])], writes=[XT[c]])
            H2 = [S.alloc(SCR + 32 * KIB + c * 4 * KIB, 4 * KIB, BF16) for c in range(NCH)]
            RSTD1 = S.alloc(ATT_OFF + 8 * KIB, 8 * KIB)
            SQF1 = [S.alloc(SCR + 64 * KIB + i * 8 * KIB, 8 * KIB) for i in range(2)]
            SSQ1 = S.alloc(SCR + 80 * KIB, 8 * KIB)
            SSB1 = S.alloc(SCR + 88 * KIB, 4 * KIB, BF16)
            for mg in range(2):
                for mm in range(4):
                    m = mg * 4 + mm
                    wo_t, wo = wload(w_out, 0, D, m * 128, 128)
                    for tb in range(NTB):
                        sl = slice(tb * TB, (tb + 1) * TB)
                        ps = next_ps()
                        mm_group(ps, ps.ap[:, :], proj_fm(wo, 0, MIX, NCH, tb), reads=MIX + wo_t)
                        S.op("dve", lambda e, ps=ps, m=m, sl=sl: e.tensor_tensor(
                            out=XT[m].ap[:, sl], in0=ps.ap[:, :], in1=XT[m].ap[:, sl], op=ALU.add),
                            reads=[ps, XT[m]], writes=[XT[m]])
                    if m >= 1:
                        stats_acc(XT[m - 1], SQF1[(m - 1) % 2], SSQ1, m - 1 == 0)
                    S.op("act", lambda e, m=m: e.activation(out=H2[m].ap[:, :], in_=XT[m].ap[:, :], func=AF.Copy,
                                                            scale=par(P_G2 + m)),
                         reads=[XT[m], PAR], writes=[H2[m]])
            stats_acc(XT[NCH - 1], SQF1[(NCH - 1) % 2], SSQ1, False)
            stats_acc_finish(SSQ1, SSB1, RSTD1)

            checkpoint(4, XT)
            checkpoint(5, H2)
            RT = [S.alloc(SCR + 64 * KIB + i * 2 * KIB, 2 * KIB) for i in range(2)]
            a_offs = [SCR + i * 2 * KIB for i in range(16)]
            a_offs += [SCR + 68 * KIB + i * 2 * KIB for i in range(12)]
            a_offs += [ATT_OFF + i * 2 * KIB for i in range(4)]
            assert len(a_offs) == 32 and SCR + 68 * KIB + 24 * KIB <= ARENA_BYTES
            osems = [S.dma_sem("os%d" % i) for i in range(2)]
            AT = None
            it = 0
            for hf in range(2):
                AT = [S.alloc(a_offs[jn], 2 * KIB, BF16) for jn in range(32)]
                for jg in range(8):
                    for jj in range(4):
                        jn = jg * 4 + jj
                        w1_t, w1 = wload(w_ff1, 0, D, jn * 128, 128)
                        for tl in range(2):
                            tb = hf * 2 + tl
                            sl = slice(tb * TB, (tb + 1) * TB)
                            ps = next_ps()
                            mm_group(ps, ps.ap[:, :], proj_fm(w1, 0, H2, NCH, tb), reads=H2 + w1_t)
                            rt = RT[it % 2]
                            it += 1
                            S.op("act", lambda e, ps=ps, rt=rt: e.activation(out=rt.ap[:, :], in_=ps.ap[:, :], func=AF.Relu),
                                 reads=[ps], writes=[rt])
                            S.op("dve", lambda e, rt=rt, sl=sl: e.tensor_tensor(
                                out=rt.ap[:, :], in0=rt.ap[:, :], in1=RSTD1.ap[:, sl], op=ALU.mult),
                                reads=[rt, RSTD1], writes=[rt])
                            S.op("dve", lambda e, rt=rt, jn=jn, tl=tl, AT=AT: e.tensor_tensor(
                                out=AT[jn].ap[:, tl * TB:(tl + 1) * TB], in0=rt.ap[:, :], in1=rt.ap[:, :], op=ALU.mult),
                                reads=[rt], writes=[AT[jn]])
                if hf == 1:
                    SQF2 = [S.alloc(SCR + 32 * KIB + i * 8 * KIB, 8 * KIB) for i in range(2)]
                    SSQ2 = S.alloc(SCR + 48 * KIB, 8 * KIB)
                    SSB2 = S.alloc(SCR + 56 * KIB, 4 * KIB, BF16)
                for m in range(NCH):
                    w2_t, w2 = wload(w_ff2, 0, 4 * D, m * 128, 128)
                    for tl in range(2):
                        tb = hf * 2 + tl
                        sl = slice(tb * TB, (tb + 1) * TB)
                        ps = next_ps()
                        terms = [(w2[:, jn, :], AT[jn].ap[:, tl * TB:(tl + 1) * TB]) for jn in range(32)]
                        mm_group(ps, ps.ap[:, :], terms, reads=AT + w2_t)
                        S.op("dve", lambda e, ps=ps, m=m, sl=sl: e.tensor_tensor(
                            out=XT[m].ap[:, sl], in0=ps.ap[:, :], in1=XT[m].ap[:, sl], op=ALU.add),
                            reads=[ps, XT[m]], writes=[XT[m]])
                    if hf == 1 and m >= 1:
                        stats_acc(XT[m - 1], SQF2[(m - 1) % 2], SSQ2, m - 1 == 0)
                if hf == 1:
                    stats_acc(XT[NCH - 1], SQF2[(NCH - 1) % 2], SSQ2, False)

            checkpoint(6, XT)
            o = SCR
            RSTD2 = S.alloc(o, 8 * KIB); o += 8 * KIB
            OUT = [S.alloc(o + i * 8 * KIB, 8 * KIB) for i in range(2)]; o += 16 * KIB
            stats_acc_finish(SSQ2, SSB2, RSTD2)
            for c in range(NCH):
                ot = OUT[c % 2]
                S.op("dve", lambda e, c=c, ot=ot: e.scalar_tensor_tensor(out=ot.ap[:, :], in0=XT[c].ap[:, :],
                                                                         scalar=par(P_GF + c), in1=RSTD2.ap[:, :],
                                                                         op0=ALU.mult, op1=ALU.mult),
                     reads=[XT[c], PAR, RSTD2], writes=[ot])
                sp_dma(osems[c % 2], [(outT[c * 128:(c + 1) * 128, :], ot.ap[:, :])], reads=[ot])
            S.final_waits = [(ds.sem, ds.count) for ds in osems]

        except _Stop:
            pass

        with nc.Block() as block:
            @block.tensor
            def _(e):
                S.emit("pe", e)

            @block.scalar
            def _(e):
                S.emit("act", e)

            @block.vector
            def _(e):
                S.emit("dve", e)

            @block.gpsimd
            def _(e):
                S.emit("pool", e)

            @block.sync
            def _(e):
                S.emit("sp", e)
    return nc


def _cols(v, n):
    return np.ascontiguousarray(np.asarray(v, np.float32).reshape(n, 128).T)


def _build_t2(rpb):
    rpb = np.asarray(rpb, np.float32)
    kc = np.arange(64)[:, None]
    qc = np.arange(64)[None, :]
    dcol = np.clip(kc - qc, -15, 15) + 15
    ws = np.clip(qc - 8, 0, 48)
    valid = (kc >= ws) & (kc < ws + 16)
    t2 = np.empty((4, 2, 64, 2, 14, 64), np.float32)
    for a in range(2):
        for e in range(14):
            g = rpb[:, e + a][:, dcol]
            g = np.where(valid[None], g, np.float32(NEG))
            t2[:, a, :, :, e, :] = g.reshape(4, 2, 64, 64).transpose(0, 2, 1, 3)
    return np.ascontiguousarray(t2.reshape(4, 128, 2 * 14 * 64))


_NC_CACHE = {}


def kernel(x, ln1_g, w_in, b_in, rpb, w_att_o, conv_w, conv_b, w_rg_a, b_rg_a, w_rg_i, b_rg_i,
           lru_lambda, w_rec_o, w_out, ln2_g, w_ff1, w_ff2, lnf_g):
    f = lambda a: np.ascontiguousarray(np.asarray(a, np.float32))
    x = f(x)
    B = x.shape[0]
    params = np.zeros((128, NPAR), np.float32)
    params[:, P_BIN:P_BIN + 44] = _cols(f(b_in)[0], 44)
    params[:, P_G1:P_G1 + 8] = _cols(f(ln1_g)[0], 8)
    params[:, P_G2:P_G2 + 8] = _cols(f(ln2_g)[0], 8)
    params[:, P_GF:P_GF + 8] = _cols(f(lnf_g), 8)
    cw = f(conv_w)[0]
    for j in range(4):
        params[:, P_CW + j * 8:P_CW + (j + 1) * 8] = _cols(cw[j], 8)
    params[:, P_CB:P_CB + 8] = _cols(f(conv_b)[0], 8)
    for d in range(2):
        params[:, P_BA + d * 8:P_BA + (d + 1) * 8] = _cols(f(b_rg_a)[0, d], 8)
        params[:, P_BI + d * 8:P_BI + (d + 1) * 8] = _cols(f(b_rg_i)[0, d], 8)
        params[:, P_LAM + d * 8:P_LAM + (d + 1) * 8] = _cols(f(lru_lambda)[0, d], 8)
    wa = f(w_rg_a)[0]
    wi = f(w_rg_i)[0]
    w_rg = np.ascontiguousarray(np.stack([wa[0], wi[0], wa[1], wi[1]], axis=0))
    t2 = _build_t2(f(rpb)[0])
    shared = {
        "w_in": f(w_in)[0], "w_att_o": f(w_att_o)[0], "w_rec_o": f(w_rec_o)[0], "w_out": f(w_out)[0],
        "w_ff1": f(w_ff1)[0], "w_ff2": f(w_ff2)[0], "w_rg": w_rg, "params": params, "t2": t2,
        "ident": np.eye(128, dtype=np.float32),
    }
    in_maps = []
    for b in range(B):
        m = dict(shared)
        m["xT"] = np.ascontiguousarray(x[b].T)
        in_maps.append(m)
    if "nc" not in _NC_CACHE:
        _NC_CACHE["nc"] = build_program()
    nc = _NC_CACHE["nc"]
    res = run_bass_kernel_spmd(nc, in_maps, core_ids=list(range(B)))
    out = np.stack([np.ascontiguousarray(r["outT"].T) for r in res.results], axis=0)
    return out.astype(np.float32)
```
